# Optimizing a Trainium2 kernel written in Bass

```python
import jax
import jax.numpy as jnp
from jax import lax
import numpy as np

D_MODEL = 4096
BATCH = 16
SEQ = 256
DEPTH = 2
DEC_BATCH = 4
DEC_SEQ = 1024
PAST_LEN = 512

GRID_W = 64
MLA_HEADS = 16
QK_NOPE = 128
QK_ROPE = 64
V_HEAD = 128
Q_LORA = 1024
KV_LORA = 512
GLA_HEADS = 8
GLA_DK = 128
GLA_DV = 256
GATE_RANK = 16
GATE_TAU = 16.0
CHUNK = 64
D_FF = 4 * D_MODEL
ROPE_BASE = 10000.0
EPS = 1e-6
Q_BLOCK = 128

MLA_OUT = MLA_HEADS * V_HEAD
GLA_QK = GLA_HEADS * GLA_DK
GLA_OUT = GLA_HEADS * GLA_DV
MIX_WIDTH = MLA_OUT + GLA_OUT
OFF_KV = Q_LORA
OFF_GQ = OFF_KV + KV_LORA + QK_ROPE
OFF_GK = OFF_GQ + GLA_QK
OFF_GV = OFF_GK + GLA_QK
OFF_GATE = OFF_GV + GLA_OUT
OFF_OG = OFF_GATE + 2 * GATE_RANK
N_IN = OFF_OG + GLA_OUT

kernel_name = 'hybrid_mla_gla_diffusion_step'


def rmsnorm(x, g):
    xf = x.astype(jnp.float32)
    y = xf * lax.rsqrt(jnp.mean(xf * xf, axis=-1, keepdims=True) + EPS)
    return (y * g.astype(jnp.float32)).astype(x.dtype)


def axial_rope(n_tok):
    rows = n_tok // GRID_W
    row = jnp.repeat(jnp.arange(rows, dtype=jnp.float32), GRID_W)
    col = jnp.tile(jnp.arange(GRID_W, dtype=jnp.float32), rows)
    n_freq = QK_ROPE // 4
    inv = jnp.power(ROPE_BASE, -jnp.arange(n_freq, dtype=jnp.float32) / n_freq)
    ang = jnp.concatenate([row[:, None] * inv, col[:, None] * inv], axis=-1)
    return jnp.cos(ang), jnp.sin(ang)


def apply_rope(x, cos, sin):
    half = QK_ROPE // 2
    x1 = x[..., :half].astype(jnp.float32)
    x2 = x[..., half:].astype(jnp.float32)
    return jnp.concatenate([x1 * cos - x2 * sin, x1 * sin + x2 * cos], axis=-1).astype(x.dtype)


def block_attention(q, k, v):
    B, Tq, H, dq = q.shape
    nb = Tq // Q_BLOCK
    qb = q.reshape(B, nb, Q_BLOCK, H, dq).transpose(1, 0, 2, 3, 4)
    scale = dq ** -0.5

    def one(qi):
        s = jnp.einsum('bqhd,bkhd->bhqk', qi, k, preferred_element_type=jnp.float32) * scale
        p = jax.nn.softmax(s, axis=-1)
        return jnp.einsum('bhqk,bkhd->bqhd', p.astype(v.dtype), v)

    o = lax.map(one, qb)
    return o.transpose(1, 0, 2, 3, 4).reshape(B, Tq, H, v.shape[-1])


def gla_scan(q, k, v, g, s0):
    B, T, H, dk = q.shape
    dv = v.shape[-1]
    n = T // CHUNK

    def to_chunks(a):
        return a.reshape(B, n, CHUNK, H, a.shape[-1]).transpose(1, 0, 3, 2, 4)

    mask = jnp.tril(jnp.ones((CHUNK, CHUNK), dtype=bool))

    def step(S, inp):
        qi, ki, vi, gi = inp
        b = jnp.cumsum(gi.astype(jnp.float32), axis=2)
        diff = b[:, :, :, None, :] - b[:, :, None, :, :]
        decay = jnp.exp(jnp.where(mask[:, :, None], diff, -jnp.inf))
        A = jnp.einsum('bhid,bhjd,bhijd->bhij', qi, ki, decay)
        o = jnp.einsum('bhij,bhjv->bhiv', A, vi) + jnp.einsum('bhid,bhdv->bhiv', qi * jnp.exp(b), S)
        b_last = b[:, :, -1:, :]
        S_new = jnp.exp(b_last[:, :, 0, :])[..., None] * S + jnp.einsum('bhjd,bhjv->bhdv', ki * jnp.exp(b_last - b), vi)
        return S_new, o

    S, o = lax.scan(step, s0.astype(jnp.float32), (to_chunks(q), to_chunks(k), to_chunks(v), to_chunks(g)))
    o = o.transpose(1, 0, 3, 2, 4).reshape(B, T, H, dv)
    return o, S


def mixer(h, P, l, rope, ctx_ckv, ctx_krope, s0_fwd, s0_bwd):
    B, T, _ = h.shape
    proj = h @ P['w_in'][l]
    q_lat, kv_lat, gq, gk, gv, g_lr, og = jnp.split(proj, [OFF_KV, OFF_GQ, OFF_GK, OFF_GV, OFF_GATE, OFF_OG], axis=-1)

    q = (rmsnorm(q_lat, P['q_a_norm'][l]) @ P['w_qb'][l]).reshape(B, T, MLA_HEADS, QK_NOPE + QK_ROPE)
    q_nope = rmsnorm(q[..., :QK_NOPE], P['q_norm_nope'][l])
    q_rope = rmsnorm(q[..., QK_NOPE:], P['q_norm_rope'][l])
    ckv = rmsnorm(kv_lat[..., :KV_LORA], P['kv_a_norm'][l])
    krope = rmsnorm(kv_lat[..., KV_LORA:], P['k_norm_rope'][l])
    if rope is None:
        keys_ckv, keys_krope = ckv, krope
    else:
        cos, sin = rope
        q_rope = apply_rope(q_rope, cos[:, None, :], sin[:, None, :])
        keys_ckv = jnp.concatenate([ckv, ctx_ckv.astype(ckv.dtype)], axis=1)
        keys_krope = jnp.concatenate([apply_rope(krope, cos, sin), ctx_krope.astype(krope.dtype)], axis=1)
    Tk = keys_ckv.shape[1]
    kv = (keys_ckv @ P['w_kvb'][l]).reshape(B, Tk, MLA_HEADS, QK_NOPE + V_HEAD)
    k_nope = rmsnorm(kv[..., :QK_NOPE], P['k_norm_nope'][l])
    v = kv[..., QK_NOPE:]
    k = jnp.concatenate([k_nope, jnp.broadcast_to(keys_krope[:, :, None, :], (B, Tk, MLA_HEADS, QK_ROPE))], axis=-1)
    o_mla = block_attention(jnp.concatenate([q_nope, q_rope], axis=-1), k, v).reshape(B, T, MLA_OUT)
    o_mla = rmsnorm(o_mla, P['mla_out_norm'][l])

    gq = gq.reshape(B, T, GLA_HEADS, GLA_DK) * (GLA_DK ** -0.5)
    gk = gk.reshape(B, T, GLA_HEADS, GLA_DK)
    gv = gv.reshape(B, T, GLA_HEADS, GLA_DV)
    g_f = jax.nn.log_sigmoid((g_lr[..., :GATE_RANK] @ P['w_gf2'][l] + P['b_gf'][l]).astype(jnp.float32)) / GATE_TAU
    g_b = jax.nn.log_sigmoid((g_lr[..., GATE_RANK:] @ P['w_gb2'][l] + P['b_gb'][l]).astype(jnp.float32)) / GATE_TAU
    g_f = g_f.reshape(B, T, GLA_HEADS, GLA_DK)
    g_b = g_b.reshape(B, T, GLA_HEADS, GLA_DK)
    if s0_fwd is None:
        s0_fwd = jnp.zeros((B, GLA_HEADS, GLA_DK, GLA_DV), jnp.float32)
        s0_bwd = jnp.zeros((B, GLA_HEADS, GLA_DK, GLA_DV), jnp.float32)
    o_f, s_f = gla_scan(gq, gk, gv, g_f, s0_fwd)
    flip = lambda a: jnp.flip(a, axis=1)
    o_b, s_b = gla_scan(flip(gq), flip(gk), flip(gv), flip(g_b), s0_bwd)
    o_gla = (o_f + flip(o_b)).astype(h.dtype)
    o_gla = rmsnorm(o_gla, P['gla_norm'][l]).reshape(B, T, GLA_OUT) * jax.nn.silu(og)

    out = jnp.concatenate([o_mla, o_gla], axis=-1) @ P['w_o'][l]
    return out, (ckv, krope, s_f, s_b)


def trunk_layer(x, cvec, P, l, rope, ctx_ckv, ctx_krope, s0_fwd, s0_bwd):
    mod = (jax.nn.silu(cvec) @ P['w_ada'][l] + P['b_ada'][l])[:, None, :]
    sh1, sc1, gt1, sh2, sc2, gt2 = jnp.split(mod, 6, axis=-1)
    h = rmsnorm(x, P['norm1'][l]) * (1 + sc1) + sh1
    mix, ctx_out = mixer(h, P, l, rope, ctx_ckv, ctx_krope, s0_fwd, s0_bwd)
    x = x + gt1 * mix
    h = rmsnorm(x, P['norm2'][l]) * (1 + sc2) + sh2
    f = jnp.square(jax.nn.relu(h @ P['w_up'][l])) @ P['w_down'][l]
    x = x + gt2 * f
    return x, ctx_out


def setup_inputs(seed: int = 0) -> dict:
    key = jax.random.key(seed)
    ks = iter(jax.random.split(key, 40))

    def nrm(shape, scale=1.0):
        return jax.random.normal(next(ks), shape, jnp.float32) * scale

    def gain(shape):
        return 1.0 + nrm(shape, 0.02)

    return {
        'x_prompt': nrm((BATCH, SEQ, D_MODEL)),
        'x_sample': nrm((DEC_BATCH, DEC_SEQ, D_MODEL)),
        'c': nrm((DEC_BATCH, D_MODEL)),
        'cache_ckv': nrm((DEC_BATCH, DEPTH, PAST_LEN, KV_LORA)),
        'cache_krope': nrm((DEC_BATCH, DEPTH, PAST_LEN, QK_ROPE)),
        'state_gla_fwd': nrm((DEC_BATCH, DEPTH, GLA_HEADS, GLA_DK, GLA_DV)),
        'state_gla_bwd': nrm((DEC_BATCH, DEPTH, GLA_HEADS, GLA_DK, GLA_DV)),
        'c_ctx': nrm((D_MODEL,)),
        'w_ada': nrm((DEPTH, D_MODEL, 6 * D_MODEL), 0.5 * D_MODEL ** -0.5),
        'b_ada': nrm((DEPTH, 6 * D_MODEL), 0.01),
        'norm1': gain((DEPTH, D_MODEL)),
        'norm2': gain((DEPTH, D_MODEL)),
        'w_in': nrm((DEPTH, D_MODEL, N_IN), D_MODEL ** -0.5),
        'q_a_norm': gain((DEPTH, Q_LORA)),
        'w_qb': nrm((DEPTH, Q_LORA, MLA_HEADS * (QK_NOPE + QK_ROPE)), Q_LORA ** -0.5),
        'kv_a_norm': gain((DEPTH, KV_LORA)),
        'w_kvb': nrm((DEPTH, KV_LORA, MLA_HEADS * (QK_NOPE + V_HEAD)), KV_LORA ** -0.5),
        'q_norm_nope': gain((DEPTH, QK_NOPE)),
        'q_norm_rope': gain((DEPTH, QK_ROPE)),
        'k_norm_nope': gain((DEPTH, QK_NOPE)),
        'k_norm_rope': gain((DEPTH, QK_ROPE)),
        'w_gf2': nrm((DEPTH, GATE_RANK, GLA_QK), GATE_RANK ** -0.5),
        'b_gf': nrm((DEPTH, GLA_QK), 0.1),
        'w_gb2': nrm((DEPTH, GATE_RANK, GLA_QK), GATE_RANK ** -0.5),
        'b_gb': nrm((DEPTH, GLA_QK), 0.1),
        'gla_norm': gain((DEPTH, GLA_DV)),
        'mla_out_norm': gain((DEPTH, MLA_OUT)),
        'w_o': nrm((DEPTH, MIX_WIDTH, D_MODEL), MIX_WIDTH ** -0.5),
        'w_up': nrm((DEPTH, D_MODEL, D_FF), D_MODEL ** -0.5),
        'w_down': nrm((DEPTH, D_FF, D_MODEL), D_FF ** -0.5),
    }


def reference(x_prompt, x_sample, c, cache_ckv, cache_krope, state_gla_fwd, state_gla_bwd, c_ctx,
              w_ada, b_ada, norm1, norm2, w_in, q_a_norm, w_qb, kv_a_norm, w_kvb,
              q_norm_nope, q_norm_rope, k_norm_nope, k_norm_rope, w_gf2, b_gf, w_gb2, b_gb,
              gla_norm, mla_out_norm, w_o, w_up, w_down):
    P = {
        'w_ada': w_ada, 'b_ada': b_ada, 'norm1': norm1, 'norm2': norm2, 'w_in': w_in,
        'q_a_norm': q_a_norm, 'w_qb': w_qb, 'kv_a_norm': kv_a_norm, 'w_kvb': w_kvb,
        'q_norm_nope': q_norm_nope, 'q_norm_rope': q_norm_rope,
        'k_norm_nope': k_norm_nope, 'k_norm_rope': k_norm_rope,
        'w_gf2': w_gf2, 'b_gf': b_gf, 'w_gb2': w_gb2, 'b_gb': b_gb,
        'gla_norm': gla_norm, 'mla_out_norm': mla_out_norm, 'w_o': w_o, 'w_up': w_up, 'w_down': w_down,
    }

    y_prompt = x_prompt
    ckvs, kropes, sfs, sbs = [], [], [], []
    for l in range(DEPTH):
        y_prompt, (ckv, krope, s_f, s_b) = trunk_layer(y_prompt, c_ctx[None, :], P, l, None, None, None, None, None)
        ckvs.append(ckv)
        kropes.append(krope)
        sfs.append(s_f)
        sbs.append(s_b)
    new_ckv = jnp.stack(ckvs, axis=1)
    new_krope = jnp.stack(kropes, axis=1)
    new_state_fwd = jnp.stack(sfs, axis=1)
    new_state_bwd = jnp.stack(sbs, axis=1)

    rope = axial_rope(x_sample.shape[1])
    y_sample = x_sample
    for l in range(DEPTH):
        y_sample, _ = trunk_layer(y_sample, c, P, l, rope, cache_ckv[:, l], cache_krope[:, l],
                                  state_gla_fwd[:, l], state_gla_bwd[:, l])

    return (y_prompt, y_sample, new_ckv, new_krope, new_state_fwd, new_state_bwd)
```

```python
import numpy as np
from contextlib import ExitStack
import ml_dtypes
import concourse.bass as bass
import concourse.mybir as mybir
from concourse.bass_utils import run_bass_kernel_spmd

F32 = mybir.dt.float32
BF16 = mybir.dt.bfloat16
AF = mybir.ActivationFunctionType
ALU = mybir.AluOpType

D = 4096
T = 1024
DEPTH = 2
NIN = 7776
DFF = 16384
EPS = 1e-6
NCORES = 2
NG = 8 // NCORES
ENGS = ("sp", "act", "dve", "pool", "pe")
PL = "dve"


class Buf:
    __slots__ = ("name", "lw", "rd", "excl")

    def __init__(self, name="", excl=False):
        self.name = name
        self.lw = None
        self.rd = []
        self.excl = excl


class Op:
    __slots__ = ("eng", "emit", "deps", "dma", "slot", "pos", "sig", "cnt", "waits", "idx")


class Prog:
    def __init__(self, nc):
        self.nc = nc
        self.ops = []
        self.streams = {e: [] for e in ENGS}
        self.slot_last = {}
        self.slot_n = {}
        self.dma_since_bar = []

    def op(self, eng, emit, reads=(), writes=(), dma=False, slot=None, extra=()):
        o = Op()
        o.eng, o.emit, o.dma, o.slot = eng, emit, dma, slot
        o.idx = len(self.ops)
        deps = set(extra)
        xr = [b for b in reads if b.excl]
        if xr:
            reads = [b for b in reads if not b.excl]
            writes = list(writes) + xr
        for b in reads:
            if b.lw is not None:
                deps.add(b.lw)
        for b in writes:
            if b.lw is not None:
                deps.add(b.lw)
            deps.update(b.rd)
        if dma:
            prev = self.slot_last.get(slot)
            if prev is not None:
                deps.add(prev)
            self.slot_last[slot] = o.idx
            self.slot_n[slot] = self.slot_n.get(slot, 0) + 1
            o.cnt = self.slot_n[slot]
            self.dma_since_bar.append(o.idx)
        deps.discard(o.idx)
        o.deps = deps
        for b in reads:
            b.rd.append(o.idx)
        for b in writes:
            b.lw = o.idx
            b.rd = []
        o.pos = len(self.streams[eng])
        o.sig = False
        self.streams[eng].append(o)
        self.ops.append(o)
        return o.idx

    def dma(self, q, out, in_, reads, writes, slot):
        return self.op(q, lambda e: e.dma_start(out=out, in_=in_), reads, writes, dma=True, slot=slot)

    def barrier(self):
        lasts = []
        for e in ENGS:
            for o in reversed(self.streams[e]):
                if o.emit is not None and not o.dma:
                    lasts.append(o.idx)
                    break
        extra = lasts + list(self.dma_since_bar)
        self.dma_since_bar = []
        for e in ENGS:
            self.op(e, None, extra=extra)

    def finalize(self, final_reads):
        self.op("sp", None, reads=final_reads, writes=(), extra=list(self.dma_since_bar))
        ops = self.ops
        seen = {e: {f: -1 for f in ENGS} for e in ENGS}
        seen_dma = {e: {} for e in ENGS}
        for o in ops:
            need = {}
            needd = {}
            for d in o.deps:
                t = ops[d]
                if t.dma:
                    if seen_dma[o.eng].get(t.slot, 0) < t.cnt and needd.get(t.slot, (0, None))[0] < t.cnt:
                        needd[t.slot] = (t.cnt, t)
                else:
                    if t.emit is None:
                        continue
                    if t.eng == "pe" and o.eng == "pe":
                        continue
                    if seen[o.eng][t.eng] < t.pos and need.get(t.eng, -1) < t.pos:
                        need[t.eng] = t.pos
            o.waits = []
            for f, p in need.items():
                t = self.streams[f][p]
                t.sig = True
                seen[o.eng][f] = p
                o.waits.append(("eng", f, t))
            for k, (v, t) in needd.items():
                seen_dma[o.eng][k] = v
                o.waits.append(("dma", k, t))
        for e in ENGS:
            c = 0
            for o in self.streams[e]:
                if o.dma:
                    continue
                if o.sig:
                    c += 1
                    o.cnt = c

    def emit(self, ctx):
        nc = self.nc
        esem = {e: ctx.enter_context(nc.semaphore("s_" + e)) for e in ENGS}
        dsem = {s: ctx.enter_context(nc.semaphore("d_" + str(s))) for s in self.slot_n}
        block = ctx.enter_context(nc.Block())
        reg = {"sp": block.sync, "act": block.scalar, "dve": block.vector, "pool": block.gpsimd,
               "pe": block.tensor}

        def mk(e):
            def body(eng):
                for o in self.streams[e]:
                    for kind, k, t in o.waits:
                        if kind == "eng":
                            eng.wait_ge(esem[k], t.cnt)
                        else:
                            eng.wait_ge(dsem[k], 16 * t.cnt)
                    if o.emit is None:
                        continue
                    ins = o.emit(eng)
                    if o.dma:
                        ins.then_inc(dsem[o.slot], 16)
                    elif o.sig:
                        ins.then_inc(esem[e], 1)
            return body

        for e in ENGS:
            reg[e](mk(e))


ARENA_F32 = 50944


class Builder:
    def __init__(self, stop_after=None, debug=False, ng=None, tiny=False):
        self.tiny = tiny
        self.stop_after = stop_after
        self.debug = debug
        self.ng = NG if ng is None else ng
        nc = self.nc = bass.Bass("TRN2", target_bir_lowering=False)
        self.P = Prog(nc)
        self.ctx = ExitStack()
        self.di = {}
        self.do = {}

    def din(self, name, shape, dt=F32):
        t = self.nc.dram_tensor(name, list(shape), dt, kind="ExternalInput").ap()
        self.di[name] = t
        return t

    def dout(self, name, shape, dt=F32):
        t = self.nc.dram_tensor(name, list(shape), dt, kind="ExternalOutput").ap()
        self.do[name] = t
        return t

    def dscr(self, name, shape, dt=F32):
        if self.debug:
            return self.dout(name, shape, dt)
        return self.nc.dram_tensor(name, list(shape), dt).ap()

    def take32(self, *shape, parts=128):
        n = int(np.prod(shape))
        assert self.off + n <= ARENA_F32, (self.off, n)
        ap = self.arena[0:parts, self.off:self.off + n]
        self.off += n
        if len(shape) == 2:
            ap = ap.rearrange("p (a b) -> p a b", a=shape[0])
        elif len(shape) == 3:
            ap = ap.rearrange("p (a b c) -> p a b c", a=shape[0], b=shape[1])
        return ap

    def take16(self, *shape, parts=128):
        n = int(np.prod(shape))
        n32 = (n + 1) // 2
        assert self.off + n32 <= ARENA_F32, (self.off, n32)
        ap = self.arena16[0:parts, 2 * self.off:2 * self.off + n]
        self.off += n32
        if len(shape) == 2:
            ap = ap.rearrange("p (a b) -> p a b", a=shape[0])
        elif len(shape) == 3:
            ap = ap.rearrange("p (a b c) -> p a b c", a=shape[0], b=shape[1])
        return ap

    def ACT(self, out, in_, func, rd, wr, scale=1.0, bias=0.0, accum=None):
        if accum is None:
            self.P.op("act", lambda e: e.activation(out, in_, func, bias=bias, scale=scale), rd, wr)
        else:
            self.P.op("act", lambda e: e.activation(out, in_, func, bias=bias, scale=scale, accum_out=accum), rd, wr)

    def TT(self, out, a, b, op, rd, wr, eng="dve"):
        self.P.op(eng, lambda e: e.tensor_tensor(out, a, b, op), rd, wr)

    def STT(self, out, in0, scalar, in1, op0, op1, rd, wr):
        self.P.op("dve", lambda e: e.scalar_tensor_tensor(out, in0, scalar, in1, op0, op1), rd, wr)

    def TS(self, out, in0, s1, op0, rd, wr, eng="dve"):
        self.P.op(eng, lambda e: e.tensor_scalar(out, in0, s1, None, op0), rd, wr)

    def CP(self, out, in_, rd, wr, eng="dve"):
        if eng == "act":
            self.ACT(out, in_, AF.Copy, rd, wr)
        else:
            self.P.op(eng, lambda e: e.tensor_copy(out, in_), rd, wr)

    def MM(self, out, lhsT, rhs, start, stop, rd, wr):
        self.P.op("pe", lambda e: e.matmul(out, lhsT, rhs, start=start, stop=stop), rd, wr)

    def TR(self, out, in_, rd, wr):
        k = in_.shape[0]
        idn = self.ident[0:k, 0:k]
        self.P.op("pe", lambda e: e.transpose(out, in_, idn), rd + [self.Bconst], wr)

    def LD(self, out, in_, wr, slot, rd=()):
        self.P.dma("sp", out, in_, list(rd), list(wr), slot)

    def LDW(self, out, in_, wr, slot):
        self.P.dma("pool", out, in_, [], list(wr), slot)

    def ST(self, out, in_, rd, wr, slot):
        self.P.dma("sp", out, in_, list(rd), list(wr), slot)

    def rsqrt(self, out, in_, n, tmp, rd, wr, wtmp):
        self.ACT(tmp, in_, AF.Ln, rd, wtmp, scale=1.0 / n, bias=self.epsc[0:in_.shape[0], :])
        self.ACT(out, tmp, AF.Exp, wtmp, wr, scale=-0.5)

    def build(self):
        nc = self.nc
        ctx = self.ctx
        P = self.P
        NGc = self.ng
        xd = self.din("x", [NGc, T, D])
        cvd = self.din("cv", [NGc, 128, 32])
        cckvd = self.din("cckv", [NGc, DEPTH, 512, 512])
        ckrd = self.din("ckr", [NGc, DEPTH, 512, 64])
        s0d = [self.din("s0f", [NGc, DEPTH, 8, 128, 256]), self.din("s0b", [NGc, DEPTH, 8, 128, 256])]
        identd = self.din("ident", [128, 128])
        ropedd = self.din("rope", [NGc, 64, 2048])
        rtd = self.din("rt", [64, 64])
        eqdd = self.din("eqm", [NGc, 4, 1024], BF16)
        fkdd = self.din("fkm", [NGc, 4, 1536], BF16)
        keepdd = self.din("keep", [NGc, 128, 16])
        tmatd = self.din("tmat", [128, 4 * 128])
        tiny = self.tiny
        w_ada = self.din("w_ada", [DEPTH, D, 512 if tiny else 6 * D])
        b_ada = self.din("b_ada", [DEPTH, 6 * D])
        norm1 = self.din("norm1", [DEPTH, D])
        norm2 = self.din("norm2", [DEPTH, D])
        w_in = self.din("w_in", [DEPTH, D, 256 if tiny else NIN])
        w_qb = self.din("w_qb", [DEPTH, 1024, 192 if tiny else 3072])
        w_kvb = self.din("w_kvb", [DEPTH, 512, 256 if tiny else 4096])
        gfmd = self.din("gfm", [DEPTH, 128, 32])
        wg2d = self.din("wg2", [DEPTH, 33, 2048])
        glan = self.din("gla_norm", [DEPTH, 256])
        w_o = self.din("w_o", [DEPTH, 2048 if tiny else D, 512 if tiny else D])
        w_up = self.din("w_up", [DEPTH, D, 256 if tiny else DFF])
        w_down = self.din("w_down", [DEPTH, 256 if tiny else DFF, D])

        yd = self.dout("y", [NGc, T, D])
        nckvd = self.dout("nckv", [NGc, DEPTH, T, 512])
        nkrd = self.dout("nkr", [NGc, DEPTH, T, 64])
        nstd = [self.dout("nsf", [NGc, DEPTH, 4, 8, 128, 256]), self.dout("nsb", [NGc, DEPTH, 4, 8, 128, 256])]
        modrow = self.dscr("modrow", [DEPTH, 6, D])
        pj = self.dscr("pj", [6144, T])
        Byall = [[Buf("y%d" % i) for i in range(8)] for _ in range(NGc)]
        Bout = [Buf("outs")]
        Bmod = [Buf("mod%d" % i) for i in range(DEPTH)]
        Bpj = Buf("pj")

        self.arena = ctx.enter_context(nc.sbuf_tensor("arena", [128, ARENA_F32], F32))
        self.arena16 = self.arena[:, :].bitcast(BF16)
        assert tuple(self.arena16.shape) == (128, 2 * ARENA_F32), self.arena16.shape
        self.off = 0
        ps = [ctx.enter_context(nc.psum_tensor("ps%d" % i, [128, 512], F32)) for i in range(8)]
        PB = [Buf("ps%d" % i, excl=True) for i in range(8)]

        self.Bconst = Bc = Buf("const")
        self.ident = ident = self.take32(128)
        tmat = self.take32(4, 128)
        rope = self.take32(2, 1024)
        rt = self.take32(64)
        keep = self.take32(16)
        ones32 = self.take32(128)
        epsc = self.take32(1)
        self.epsc = epsc[:, 0:1]
        onec = self.take32(1)
        cvt = self.take32(32)
        gfm = self.take32(32)
        wg2 = self.take32(2048)
        gnbc = self.take32(256)
        ones16 = self.take16(128)
        kraug = self.take16(1536)
        qraug = self.take16(1024)
        sT = self.take16(32)
        Bgl = Buf("gfm")
        Bkr = Buf("kraug")
        Bqr = Buf("qraug")
        base_off = self.off

        LDc = lambda out, in_, slot: self.LD(out, in_, [Bc], slot)
        LDc(ident, identd, "c0")
        LDc(tmat, tmatd.rearrange("p (a b) -> p a b", a=4), "c1")
        LDc(rt[0:64], rtd, "c3")
        P.op("dve", lambda e: e.memset(ones32, 1.0), [], [Bc])
        P.op("dve", lambda e: e.memset(ones16, 1.0), [], [Bc])
        P.op("dve", lambda e: e.memset(epsc, EPS), [], [Bc])
        P.op("dve", lambda e: e.memset(onec, 1.0), [], [Bc])
        P.barrier()
        cos2 = rope[0:64, 0, :]
        sin2 = rope[0:64, 1, :]
        one1 = onec[:, 0:1]

        def phase_reset():
            P.barrier()
            self.off = base_off

        def phase_ada(l):
            slab = [self.take16(32, 512) for _ in range(2)]
            Bsl = [Buf(), Buf()]
            modseg = self.take32(D)
            bseg = self.take32(D)
            nrow = self.take32(D)
            Bms, Bbs, Bnr = Buf(), Buf(), Buf()
            it = 0
            for s in range(6):
                self.LD(bseg[0:1, :], b_ada[l:l + 1, s * D:(s + 1) * D], [Bbs], "a_b")
                if s in (1, 4):
                    nr = norm1 if s == 1 else norm2
                    self.LD(nrow[0:1, :], nr[l:l + 1, :], [Bnr], "a_n")
                for nt in range(8):
                    c0 = s * D + nt * 512
                    b = it % 2
                    self.LDW(slab[b], (w_ada[l, :, 0:512] if tiny else w_ada[l, :, c0:c0 + 512]).rearrange("(kc p) n -> p kc n", p=128), [Bsl[b]], "wa%d" % b)
                    pb = it % 4
                    for kc in range(32):
                        self.MM(ps[pb][0:1, :], sT[:, kc:kc + 1], slab[b][:, kc, :], kc == 0, kc == 31, [Bsl[b], Bc], [PB[pb]])
                    self.TT(modseg[0:1, nt * 512:(nt + 1) * 512], ps[pb][0:1, :], bseg[0:1, nt * 512:(nt + 1) * 512], ALU.add,
                            [PB[pb], Bbs], [Bms])
                    it += 1
                if s in (1, 4):
                    self.STT(modseg[0:1, :], modseg[0:1, :], 1.0, nrow[0:1, :], ALU.add, ALU.mult, [Bms, Bnr], [Bms])
                self.ST(modrow[l, s:s + 1, :], modseg[0:1, :], [Bms], [Bmod[l]], "a_o")

        def phase_norm(l, which, src, src_bufs, hT, BhT, tiles):
            so = 0 if which == 0 else 3
            Gbc = self.take32(D)
            SHbc = self.take32(D)
            xt = [self.take32(D) for _ in range(2)]
            ht = self.take32(D)
            junk = self.take16(D)
            ss = self.take32(8)
            lt = self.take32(8)
            rs = self.take32(8)
            BG, Bxt, Bht, Bss = Buf(), [Buf(), Buf()], Buf(), [Buf() for _ in range(8)]
            self.LD(Gbc, modrow[l, so + 1:so + 2, :].partition_broadcast(128), [BG], "n_g", rd=[Bmod[l]])
            self.LD(SHbc, modrow[l, so:so + 1, :].partition_broadcast(128), [BG], "n_s", rd=[Bmod[l]])
            for j, t in enumerate(tiles):
                b = j % 2
                self.LD(xt[b], src[t * 128:(t + 1) * 128, :], [Bxt[b]], "n_x%d" % b, rd=src_bufs)
                self.ACT(junk, xt[b], AF.Square, [Bxt[b]], [Bss[j]], accum=ss[:, j:j + 1])
                self.rsqrt(rs[:, j:j + 1], ss[:, j:j + 1], D, lt[:, j:j + 1], [Bss[j], Bc], [Bss[j]], [Bss[j]])
                self.STT(ht, xt[b], rs[:, j:j + 1], Gbc, ALU.mult, ALU.mult, [Bxt[b], Bss[j], BG], [Bht])
                self.TT(ht, ht, SHbc, ALU.add, [Bht, BG], [Bht], eng=PL)
                for g in range(8):
                    pb = g
                    for q in range(4):
                        kc = 4 * g + q
                        self.TR(ps[pb][:, q * 128:(q + 1) * 128], ht[:, kc * 128:(kc + 1) * 128], [Bht], [PB[pb]])
                    self.CP(hT[:, 4 * g:4 * g + 4, j * 128:(j + 1) * 128], ps[pb][:, :].rearrange("p (a b) -> p a b", a=4),
                            [PB[pb]], [BhT[j]], eng=("act" if g % 2 else "dve"))

        def phase_b1(l, hT, BhT, st):
            wsl = [self.take16(32, 256) for _ in range(2)]
            Bws = [Buf(), Buf()]
            tq = [self.take32(512) for _ in range(2)]
            Btq = [Buf(), Buf()]
            stage = [self.take32(512) for _ in range(4)]
            Bstg = [Buf() for _ in range(4)]
            segs = [("q", 0, 1024), ("ckv", 1024, 512), ("kr", 1536, 64), ("gq", 1600, 1024), ("gk", 2624, 1024),
                    ("gv", 3648, 2048), ("gate", 5696, 32), ("og", 5728, 2048)]
            si = 0
            ei = 0
            for name, c0, n in segs:
                for o in range(0, n, 256):
                    w = min(256, n - o)
                    b = si % 2
                    self.LDW(wsl[b][:, :, 0:w], (w_in[l, :, 0:w] if tiny else w_in[l, :, c0 + o:c0 + o + w]).rearrange("(kc p) n -> p kc n", p=128),
                             [Bws[b]], "wi%d" % b)
                    nch = (w + 127) // 128
                    for ch in range(nch):
                        m = min(128, w - ch * 128)
                        for half in range(2):
                            pb = b * 4 + ch * 2 + half
                            for kc in range(32):
                                self.MM(ps[pb][0:m, :], wsl[b][:, kc, ch * 128:ch * 128 + m], hT[:, kc, half * 512:(half + 1) * 512],
                                        kc == 0, kc == 31, [Bws[b]] + BhT, [PB[pb]])
                    for ch in range(nch):
                        m = min(128, w - ch * 128)
                        col = o + ch * 128
                        c = col // 128
                        for half in range(2):
                            pb = b * 4 + ch * 2 + half
                            hs = slice(half * 512, (half + 1) * 512)
                            src = ps[pb][0:m, :]
                            e1 = "act" if (ei % 2 or name in ("q", "ckv", "kr")) else "dve"
                            ei += 1
                            if name == "q":
                                self.CP(st["qnT"][:, c, hs], src, [PB[pb]], [st["Bq"][c]], eng=e1)
                                if c == 0:
                                    self.ACT(st["sqq"][:, hs], src, AF.Square, [PB[pb]], [st["Bsqq"][half]])
                                else:
                                    k = ei % 2
                                    self.ACT(tq[k], src, AF.Square, [PB[pb]], [Btq[k]])
                                    self.TT(st["sqq"][:, hs], st["sqq"][:, hs], tq[k], ALU.add, [Btq[k], st["Bsqq"][half]],
                                            [st["Bsqq"][half]], eng=PL)
                            elif name == "ckv":
                                self.CP(st["ckvraw"][:, c, hs], src, [PB[pb]], [st["Bcr"][c]], eng=e1)
                                if c == 0:
                                    self.ACT(st["sqc"][:, hs], src, AF.Square, [PB[pb]], [st["Bsqc"][half]])
                                else:
                                    k = ei % 2
                                    self.ACT(tq[k], src, AF.Square, [PB[pb]], [Btq[k]])
                                    self.TT(st["sqc"][:, hs], st["sqc"][:, hs], tq[k], ALU.add, [Btq[k], st["Bsqc"][half]],
                                            [st["Bsqc"][half]], eng=PL)
                            elif name == "kr":
                                self.CP(st["krraw"][0:64, hs], src, [PB[pb]], [st["Bkraw"]], eng=e1)
                                self.ACT(st["sqk"][0:64, hs], src, AF.Square, [PB[pb]], [st["Bsqk"]])
                            elif name == "gate":
                                self.CP(st["glrT"][0:32, hs], src, [PB[pb]], [st["Bglr"]], eng=e1)
                            else:
                                base = {"gq": 0, "gk": 1024, "gv": 2048, "og": 4096}[name]
                                k = ei % 4
                                self.CP(stage[k], src, [PB[pb]], [Bstg[k]], eng=e1)
                                self.ST(pj[base + col:base + col + 128, hs], stage[k], [Bstg[k]], [Bpj], "pj%d" % k)
                    si += 1

        def phase_b2(l, st):
            qnT, ckvraw, krraw, ckvnT = st["qnT"], st["ckvraw"], st["krraw"], st["ckvnT"]
            rq = self.take32(T)
            tmp = self.take32(T)
            ostage = [self.take32(512) for _ in range(2)]
            kstage = self.take32(8, 64)
            cst = self.take32(4, 512)
            cks = self.take32(4, 64)
            t1 = self.take32(512)
            t2 = self.take32(512)
            Brq, Btmp, Bos, Bks, Bcst, Bcks, Bt1, Bt2 = Buf(), Buf(), [Buf(), Buf()], Buf(), Buf(), Buf(), Buf(), Buf()
            self.LD(cst, cckv[l].rearrange("(t p) f -> p t f", p=128), [Bcst], "b2c")
            self.LD(cks, ckr[l].rearrange("(t p) f -> p t f", p=128), [Bcks], "b2k")
            for half in range(2):
                hs = slice(half * 512, (half + 1) * 512)
                self.MM(ps[half][:, :], ones32, st["sqq"][:, hs], True, True, [Bc, st["Bsqq"][half]], [PB[half]])
                self.rsqrt(rq[:, hs], ps[half][:, :], 1024, tmp[:, hs], [PB[half], Bc], [Brq], [Btmp])
            for c in range(8):
                self.STT(qnT[:, c, :], qnT[:, c, :], gfm[:, c:c + 1], rq, ALU.mult, ALU.mult, [st["Bq"][c], Bgl, Brq], [st["Bq"][c]])
            for half in range(2):
                hs = slice(half * 512, (half + 1) * 512)
                self.MM(ps[2 + half][:, :], ones32, st["sqc"][:, hs], True, True, [Bc, st["Bsqc"][half]], [PB[2 + half]])
                self.rsqrt(rq[:, hs], ps[2 + half][:, :], 512, tmp[:, hs], [PB[2 + half], Bc, Brq], [Brq], [Btmp])
            for c in range(4):
                self.STT(ckvraw[:, c, :], ckvraw[:, c, :], gfm[:, 8 + c:9 + c], rq, ALU.mult, ALU.mult, [st["Bcr"][c], Bgl, Brq],
                         [st["Bcr"][c]])
                self.CP(ckvnT[:, c, 0:T], ckvraw[:, c, :], [st["Bcr"][c]], [st["Bckv"]], eng="act")
            for t in range(8):
                pb = 4 + t % 4
                for c in range(4):
                    self.TR(ps[pb][:, c * 128:(c + 1) * 128], ckvraw[:, c, t * 128:(t + 1) * 128], [st["Bcr"][c]], [PB[pb]])
                k = t % 2
                self.CP(ostage[k], ps[pb][:, :], [PB[pb]], [Bos[k]], eng=("act" if t % 2 else "dve"))
                self.ST(nckv[l, t * 128:(t + 1) * 128, :], ostage[k], [Bos[k]], Bout, "b2o%d" % k)
            for half in range(2):
                hs = slice(half * 512, (half + 1) * 512)
                self.MM(ps[half][0:64, :], ones32[0:64, 0:64], st["sqk"][0:64, hs], True, True, [Bc, st["Bsqk"]], [PB[half]])
                self.rsqrt(rq[0:64, hs], ps[half][0:64, :], 64, tmp[0:64, hs], [PB[half], Bc, Brq], [Brq], [Btmp])
            self.STT(krraw[0:64, :], krraw[0:64, :], gfm[0:64, 15:16], rq[0:64, :], ALU.mult, ALU.mult, [st["Bkraw"], Bgl, Brq],
                     [st["Bkraw"]])
            for t in range(8):
                self.TR(ps[2][:, t * 64:(t + 1) * 64], krraw[0:64, t * 128:(t + 1) * 128], [st["Bkraw"]], [PB[2]])
            self.CP(kstage, ps[2][:, :].rearrange("p (a b) -> p a b", a=8), [PB[2]], [Bks])
            self.ST(nkr[l].rearrange("(t p) d -> p t d", p=128), kstage, [Bks], Bout, "b2r")
            for half in range(2):
                hs = slice(half * 512, (half + 1) * 512)
                self.MM(ps[3][0:64, :], rt[0:64, :], krraw[0:64, hs], True, True, [Bc, st["Bkraw"]], [PB[3]])
                self.TT(t1[0:64, :], ps[3][0:64, :], sin2[:, hs], ALU.mult, [PB[3], Bc], [Bt1])
                self.TT(t2[0:64, :], krraw[0:64, hs], cos2[:, hs], ALU.mult, [st["Bkraw"], Bc], [Bt2], eng=PL)
                self.TT(kraug[0:64, hs], t1[0:64, :], t2[0:64, :], ALU.add, [Bt1, Bt2], [Bkr])
            for tt in range(4):
                pb = 4 + tt
                for c in range(4):
                    self.TR(ps[pb][:, c * 128:(c + 1) * 128], cst[:, tt, c * 128:(c + 1) * 128], [Bcst], [PB[pb]])
                self.CP(ckvnT[:, :, T + tt * 128:T + (tt + 1) * 128], ps[pb][:, :].rearrange("p (a b) -> p a b", a=4), [PB[pb]],
                        [st["Bckv"]], eng=("act" if tt % 2 else "dve"))
            for tt in range(4):
                self.TR(ps[0][0:64, tt * 128:(tt + 1) * 128], cks[:, tt, :], [Bcks], [PB[0]])
            self.CP(kraug[0:64, T:T + 512], ps[0][0:64, :], [PB[0]], [Bkr])

        def phase_b4(l, st, OT, BOT):
            qnT, ckvnT = st["qnT"], st["ckvnT"]
            Bqall = st["Bq"]
            wq = [self.take16(8, 192) for _ in range(2)]
            wkv = [self.take16(4, 256) for _ in range(2)]
            Bwq, Bwkv = [Buf(), Buf()], [Buf(), Buf()]
            qhT = self.take16(T)
            khT = self.take16(1536)
            vh = self.take16(12, 128)
            sqb = [self.take16(512) for _ in range(2)]
            pT = [self.take16(512) for _ in range(3)]
            tln = self.take32(512)
            rn = self.take32(512)
            qrn = self.take32(512)
            t1 = self.take32(512)
            t2 = self.take32(512)
            rcp = self.take32(512)
            oh = self.take32(512)
            tq = self.take32(512)
            sqo = self.take32(T)
            rmla = st["rmla"]
            Bqh, Bkh, Bvh, Bsqb, BpT = Buf(), Buf(), Buf(), [Buf(), Buf()], [Buf() for _ in range(3)]
            Btln, Brn, Bqrn, Bt1, Bt2, Brcp, Boh, Btq, Bsqo = [Buf() for _ in range(9)]
            qi = 0

            def normstat(src, m, pbs, n):
                nonlocal qi
                k = qi % 2
                qi += 1
                self.ACT(sqb[k][0:m, :], src, AF.Square, [pbs], [Bsqb[k]])
                self.MM(ps[3][0:m, :], ones16[0:m, 0:m], sqb[k][0:m, :], True, True, [Bc, Bsqb[k]], [PB[3]])
                self.rsqrt(rn[0:m, :], ps[3][0:m, :], n, tln[0:m, :], [PB[3], Bc], [Brn], [Btln])

            for h in range(16):
                hb = h % 2
                self.LDW(wq[hb], (w_qb[l, :, 0:192] if tiny else w_qb[l, :, h * 192:(h + 1) * 192]).rearrange("(kc p) n -> p kc n", p=128), [Bwq[hb]], "wq%d" % hb)
                self.LDW(wkv[hb], (w_kvb[l, :, 0:256] if tiny else w_kvb[l, :, h * 256:(h + 1) * 256]).rearrange("(kc p) n -> p kc n", p=128), [Bwkv[hb]],
                         "wk%d" % hb)
                for half in range(2):
                    hs = slice(half * 512, (half + 1) * 512)
                    pn = half
                    for kc in range(8):
                        self.MM(ps[pn][:, :], wq[hb][:, kc, 0:128], qnT[:, kc, hs], kc == 0, kc == 7, [Bwq[hb]] + Bqall, [PB[pn]])
                    for kc in range(8):
                        self.MM(ps[2][0:64, :], wq[hb][:, kc, 128:192], qnT[:, kc, hs], kc == 0, kc == 7, [Bwq[hb]] + Bqall, [PB[2]])
                    normstat(ps[pn][:, :], 128, PB[pn], 128)
                    self.STT(qhT[:, hs], ps[pn][:, :], gfm[:, 12:13], rn, ALU.mult, ALU.mult, [PB[pn], Bgl, Brn], [Bqh])
                    normstat(ps[2][0:64, :], 64, PB[2], 64)
                    self.STT(qrn[0:64, :], ps[2][0:64, :], gfm[0:64, 13:14], rn[0:64, :], ALU.mult, ALU.mult, [PB[2], Bgl, Brn], [Bqrn])
                    self.MM(ps[3][0:64, :], rt[0:64, :], qrn[0:64, :], True, True, [Bc, Bqrn], [PB[3]])
                    self.TT(t1[0:64, :], ps[3][0:64, :], sin2[:, hs], ALU.mult, [PB[3], Bc], [Bt1])
                    self.TT(t2[0:64, :], qrn[0:64, :], cos2[:, hs], ALU.mult, [Bqrn, Bc], [Bt2], eng=PL)
                    self.TT(qraug[0:64, hs], t1[0:64, :], t2[0:64, :], ALU.add, [Bt1, Bt2], [Bqr])
                for k3 in range(3):
                    pk = k3 % 2
                    ks = slice(k3 * 512, (k3 + 1) * 512)
                    for kc in range(4):
                        self.MM(ps[pk][:, :], wkv[hb][:, kc, 0:128], ckvnT[:, kc, ks], kc == 0, kc == 3, [Bwkv[hb], st["Bckv"]], [PB[pk]])
                    normstat(ps[pk][:, :], 128, PB[pk], 128)
                    self.STT(khT[:, ks], ps[pk][:, :], gfm[:, 14:15], rn, ALU.mult, ALU.mult, [PB[pk], Bgl, Brn], [Bkh])
                for g in range(3):
                    for j in range(4):
                        kt = 4 * g + j
                        for kc in range(4):
                            self.MM(ps[2][:, j * 128:(j + 1) * 128], ckvnT[:, kc, kt * 128:(kt + 1) * 128], wkv[hb][:, kc, 128:256],
                                    kc == 0, kc == 3, [Bwkv[hb], st["Bckv"]], [PB[2]])
                    self.CP(vh[:, 4 * g:4 * g + 4, :], ps[2][:, :].rearrange("p (a b) -> p a b", a=4), [PB[2]], [Bvh], eng="act")
                for half in range(2):
                    hs = slice(half * 512, (half + 1) * 512)
                    for kt in range(12):
                        pss = 4 + kt % 2
                        pk = kt % 3
                        self.MM(ps[pss][:, :], khT[:, kt * 128:(kt + 1) * 128], qhT[:, hs], True, False, [Bkh, Bqh], [PB[pss]])
                        self.MM(ps[pss][:, :], kraug[0:68, kt * 128:(kt + 1) * 128], qraug[0:68, hs], False, True, [Bkr, Bqr, Bc], [PB[pss]])
                        self.ACT(pT[pk], ps[pss][:, :], AF.Exp, [PB[pss]], [BpT[pk]], scale=192.0 ** -0.5)
                        self.MM(ps[6][:, :], vh[:, kt, :], pT[pk], kt == 0, kt == 11, [Bvh, BpT[pk]], [PB[6]])
                        self.MM(ps[7][:, :], ones16, pT[pk], kt == 0, kt == 11, [Bc, BpT[pk]], [PB[7]])
                    self.P.op("dve", lambda e: e.reciprocal(rcp, ps[7][:, :]), [PB[7]], [Brcp])
                    self.TT(oh, ps[6][:, :], rcp, ALU.mult, [PB[6], Brcp], [Boh])
                    if h == 0:
                        self.ACT(sqo[:, hs], oh, AF.Square, [Boh], [Bsqo])
                    else:
                        self.ACT(tq, oh, AF.Square, [Boh], [Btq])
                        self.TT(sqo[:, hs], sqo[:, hs], tq, ALU.add, [Btq, Bsqo], [Bsqo], eng=PL)
                    self.TS(OT[:, h, hs], oh, gfm[:, 16 + h:17 + h], ALU.mult, [Boh, Bgl], [BOT])
            for t in range(8):
                self.MM(ps[3][:, t:t + 1], sqo[:, t * 128:(t + 1) * 128], ones32[:, 0:1], True, True, [Bsqo, Bc], [PB[3]])
            self.rsqrt(rmla, ps[3][:, 0:8], 2048, tln[:, 0:8], [PB[3], Bc], [st["Brmla"]], [Btln])

        def phase_c(l, st, OT, BOT):
            glrT = st["glrT"]
            sp = self.take32(2, 8, 128)
            Bsp = [[Buf() for _ in range(8)] for _ in range(2)]
            ld = self.take32(6, T)
            Bld = Buf()
            ktok = self.take32(8, 128)
            ogtok = self.take32(8, 256)
            oacc = self.take32(8, 256)
            vtok = self.take16(8, 256)
            Bkt, Bogt, Boa, Bvt = Buf(), Buf(), [Buf() for _ in range(8)], Buf()
            S = [[self.take32(256) for _ in range(2)] for _ in range(2)]
            Sbf = [[self.take16(256) for _ in range(2)] for _ in range(2)]
            BS = [[Buf(), Buf()], [Buf(), Buf()]]
            BSb = [[Buf(), Buf()], [Buf(), Buf()]]
            E = [self.take32(128) for _ in range(2)]
            Ei = [self.take32(128) for _ in range(2)]
            Ed = [self.take32(128) for _ in range(2)]
            ebk = [self.take32(1) for _ in range(2)]
            Qp = [self.take16(128) for _ in range(2)]
            Kpp = [self.take16(128) for _ in range(2)]
            Kp = [self.take16(128) for _ in range(2)]
            ATm = [self.take16(128) for _ in range(2)]
            BE, BEi, BEd, Bebk, BQp, BKpp, BKp, BATm = [[Buf(), Buf()] for _ in range(8)]
            ssg = self.take32(8)
            ltg = self.take32(8)
            rg = self.take32(8)
            sg = self.take32(256)
            tg = self.take32(256)
            Bssg, Bsg, Btg = [Buf() for _ in range(8)], Buf(), Buf()
            junk = self.take16(256)
            gtmp = self.take32(512)
            Bgt = Buf()
            bA = [0, 2]
            bB = [1, 3]
            PA = [[PB[bA[d]]] * 3 for d in range(2)]
            PBo = [[PB[bB[d]]] * 2 for d in range(2)]
            P.op("dve", lambda e: e.memset(glrT[32:33, :], 1.0), [st["Bglr"]], [st["Bglr"]])
            QS = 128.0 ** -0.5
            for h in range(8):
                self.LD(ld[:, 0, :], pj[h * 128:(h + 1) * 128, :], [Bld], "c_l0", rd=[Bpj])
                self.LD(ld[:, 1, :], pj[1024 + h * 128:1024 + (h + 1) * 128, :], [Bld], "c_l1", rd=[Bpj])
                self.LD(ld[:, 2:4, :], pj[2048 + h * 256:2048 + (h + 1) * 256, :].rearrange("(a p) t -> p a t", p=128), [Bld], "c_l2", rd=[Bpj])
                self.LD(ld[:, 4:6, :], pj[4096 + h * 256:4096 + (h + 1) * 256, :].rearrange("(a p) t -> p a t", p=128), [Bld], "c_l3", rd=[Bpj])
                qT = ld[:, 0, :]
                kT = ld[:, 1, :]
                for d in range(2):
                    for g in range(2):
                        pb = 4 + g
                        for j in range(4):
                            c = 4 * g + j
                            self.MM(ps[pb][:, j * 128:(j + 1) * 128], glrT[0:33, c * 128:(c + 1) * 128],
                                    wg2[0:33, d * 1024 + h * 128:d * 1024 + (h + 1) * 128], True, True, [st["Bglr"], Bgl], [PB[pb]])
                        self.ACT(gtmp, ps[pb][:, :], AF.Exp, [PB[pb]], [Bgt], scale=-1.0)
                        self.ACT(sp[:, d, 4 * g:4 * g + 4, :], gtmp.rearrange("p (a b) -> p a b", a=4), AF.Ln, [Bgt, Bc],
                                 [Bsp[d][4 * g + j] for j in range(4)], bias=one1)
                for g in range(2):
                    pb = 4 + g
                    for j in range(4):
                        c = 4 * g + j
                        self.TR(ps[pb][:, j * 128:(j + 1) * 128], kT[:, c * 128:(c + 1) * 128], [Bld], [PB[pb]])
                    self.CP(ktok[:, 4 * g:4 * g + 4, :], ps[pb][:, :].rearrange("p (a b) -> p a b", a=4), [PB[pb]], [Bkt], eng="act")
                for c in range(8):
                    pb = 6 + c % 2
                    for a in range(2):
                        self.TR(ps[pb][:, a * 128:(a + 1) * 128], ld[:, 2 + a, c * 128:(c + 1) * 128], [Bld], [PB[pb]])
                        self.TR(ps[pb][:, 256 + a * 128:256 + (a + 1) * 128], ld[:, 4 + a, c * 128:(c + 1) * 128], [Bld], [PB[pb]])
                    self.CP(vtok[:, c, :], ps[pb][:, 0:256], [PB[pb]], [Bvt], eng="dve")
                    self.CP(ogtok[:, c, :], ps[pb][:, 256:512], [PB[pb]], [Bogt], eng="act")
                cur = [0, 0]
                for d in range(2):
                    first = 0 if d == 0 else 7
                    self.LD(S[d][0], s0[d][l, h], [BS[d][0]], "c_s%d" % d)
                    self.ACT(Sbf[d][0], S[d][0], AF.Copy, [BS[d][0], Bc], [BSb[d][0]], scale=keep[:, d * 8 + first:d * 8 + first + 1])
                for step in range(8):
                    for d in range(2):
                        c = step if d == 0 else 7 - step
                        cs = slice(c * 128, (c + 1) * 128)
                        hcols = slice(0, 128)
                        a, b_ = bA[d], bB[d]
                        so, sn = cur[d], 1 - cur[d]
                        self.MM(ps[a][:, 0:128], sp[:, d, c, hcols], tmat[:, d, :], True, True, [Bsp[d][c], Bc], [PA[d][0]])
                        self.MM(ps[a][:, 128:256], tmat[:, 2 + d, :], sp[:, d, c, hcols], True, True, [Bsp[d][c], Bc], [PA[d][1]])
                        self.ACT(E[d], ps[a][:, 0:128], AF.Exp, [PA[d][0]], [BE[d]], scale=-1.0 / 16)
                        self.ACT(Ei[d], ps[a][:, 0:128], AF.Exp, [PA[d][0]], [BEi[d]], scale=1.0 / 16)
                        self.ACT(Ed[d], ps[a][:, 128:256], AF.Exp, [PA[d][1]], [BEd[d]], scale=-1.0 / 16)
                        self.STT(Qp[d], qT[:, cs], QS, E[d], ALU.mult, ALU.mult, [Bld, BE[d]], [BQp[d]])
                        self.TT(Kpp[d], kT[:, cs], Ei[d], ALU.mult, [Bld, BEi[d]], [BKpp[d]], eng=PL)
                        self.TT(Kp[d], ktok[:, c, :], Ed[d], ALU.mult, [Bkt, BEd[d]], [BKp[d]])
                        self.MM(ps[a][:, 256:384], Kpp[d], Qp[d], True, True, [BKpp[d], BQp[d]], [PA[d][2]])
                        self.TT(ATm[d], ps[a][:, 256:384], tmat[:, d, :], ALU.mult, [PA[d][2], Bc], [BATm[d]])
                        self.MM(ps[b_][:, 0:256], ATm[d], vtok[:, c, :], True, False, [BATm[d], Bvt], [PBo[d][0]])
                        self.MM(ps[b_][:, 0:256], Qp[d], Sbf[d][so], False, True, [BQp[d], BSb[d][so]], [PBo[d][0]])
                        first_writer = (d == 0) == (c < 4)
                        if first_writer:
                            self.CP(oacc[:, c, :], ps[b_][:, 0:256], [PBo[d][0]], [Boa[c]], eng="act")
                        else:
                            self.TT(oacc[:, c, :], oacc[:, c, :], ps[b_][:, 0:256], ALU.add, [PBo[d][0], Boa[c]], [Boa[c]])
                        self.MM(ps[b_][:, 256:512], Kp[d], vtok[:, c, :], True, True, [BKp[d], Bvt], [PBo[d][1]])
                        lastcol = 127 if d == 0 else 0
                        self.TT(ebk[d], E[d][:, lastcol:lastcol + 1], keep[:, d * 8 + c:d * 8 + c + 1], ALU.mult, [BE[d], Bc], [Bebk[d]])
                        self.STT(S[d][sn], S[d][so], ebk[d], ps[b_][:, 256:512], ALU.mult, ALU.add, [BS[d][so], Bebk[d], PBo[d][1]],
                                 [BS[d][sn]])
                        cur[d] = sn
                        seq_end = (c % 2 == 1) if d == 0 else (c % 2 == 0)
                        if seq_end:
                            self.ST(nst[d][l, c // 2, h], S[d][sn], [BS[d][sn]], Bout, "c_o%d" % d)
                        if step < 7:
                            nxt = c + 1 if d == 0 else c - 1
                            self.ACT(Sbf[d][sn], S[d][sn], AF.Copy, [BS[d][sn], Bc], [BSb[d][sn]], scale=keep[:, d * 8 + nxt:d * 8 + nxt + 1])
                for c in range(8):
                    self.ACT(junk, oacc[:, c, :], AF.Square, [Boa[c]], [Bssg[c]], accum=ssg[:, c:c + 1])
                    self.rsqrt(rg[:, c:c + 1], ssg[:, c:c + 1], 256, ltg[:, c:c + 1], [Bssg[c], Bc], [Bssg[c]], [Bssg[c]])
                    self.ACT(sg, ogtok[:, c, :], AF.Silu, [Bogt], [Bsg])
                    self.STT(tg, oacc[:, c, :], rg[:, c:c + 1], gnbc, ALU.mult, ALU.mult, [Boa[c], Bssg[c], Bgl], [Btg])
                    self.TT(tg, tg, sg, ALU.mult, [Btg, Bsg], [Btg], eng=PL)
                    pb = 4 + c % 2
                    for a in range(2):
                        self.TR(ps[pb][:, a * 128:(a + 1) * 128], tg[:, a * 128:(a + 1) * 128], [Btg], [PB[pb]])
                    self.CP(OT[:, 2 * h:2 * h + 2, c * 128:(c + 1) * 128], ps[pb][:, 0:256].rearrange("p (a b) -> p a b", a=2),
                            [PB[pb]], [BOT], eng="dve")

        def phase_d(l, part, src, src_is_y, OT, BOT, st):
            keep_off = self.off
            self.off = self.hreg
            wsl = [self.take16(16, 512) for _ in range(2)]
            self.off = keep_off
            Bws = [Buf(), Buf()]
            GTbc = self.take32(D)
            BGT = Buf()
            xs = [self.take32(8, 512) for _ in range(2)]
            Bxs = [Buf(), Buf()]
            tmp = [self.take32(512) for _ in range(2)]
            Btmp = [Buf(), Buf()]
            rmla = st["rmla"]
            self.LD(GTbc, modrow[l, 2:3, :].partition_broadcast(128), [BGT], "d_g", rd=[Bmod[l]])

            def load(s):
                rd = [By[s]] if src_is_y else []
                self.LD(xs[s % 2], src[:, s * 512:(s + 1) * 512].rearrange("(t p) n -> p t n", p=128), [Bxs[s % 2]], "d_x%d" % (s % 2), rd=rd)
            load(0)
            for s in range(8):
                b = s % 2
                self.LDW(wsl[b], (w_o[l, 0:2048, 0:512] if tiny else w_o[l, part * 2048:(part + 1) * 2048, s * 512:(s + 1) * 512]).rearrange("(kc p) n -> p kc n", p=128),
                         [Bws[b]], "wo%d" % b)
                if s + 1 < 8:
                    load(s + 1)
                for tt in range(8):
                    pb = tt
                    for kc in range(16):
                        self.MM(ps[pb][:, :], OT[:, kc, tt * 128:(tt + 1) * 128], wsl[b][:, kc, :], kc == 0, kc == 15, [BOT, Bws[b]], [PB[pb]])
                    k = tt % 2
                    if part == 0:
                        self.STT(tmp[k], ps[pb][:, :], rmla[:, tt:tt + 1], GTbc[:, s * 512:(s + 1) * 512], ALU.mult, ALU.mult,
                                 [PB[pb], st["Brmla"], BGT], [Btmp[k]])
                    else:
                        self.TT(tmp[k], ps[pb][:, :], GTbc[:, s * 512:(s + 1) * 512], ALU.mult, [PB[pb], BGT], [Btmp[k]])
                    self.TT(xs[b][:, tt, :], xs[b][:, tt, :], tmp[k], ALU.add, [Btmp[k], Bxs[b]], [Bxs[b]], eng=PL)
                self.ST(y[:, s * 512:(s + 1) * 512].rearrange("(t p) n -> p t n", p=128), xs[b], [Bxs[b]], [By[s]], "d_o%d" % b)

        def phase_e(l):
            for mt in range(2):
                h2T = self.take16(32, 512)
                Bh2 = [Buf() for _ in range(4)]
                fstart = self.off
                facc = self.take32(4, D)
                Bfa = [Buf() for _ in range(4)]
                mark = self.off
                self.off = fstart
                phase_norm(l, 1, y, By, h2T, Bh2, [mt * 4 + j for j in range(4)])
                P.barrier()
                self.off = mark
                wu = [self.take16(32, 256) for _ in range(2)]
                wd = [self.take16(2, D) for _ in range(2)]
                Bwu, Bwd = [Buf(), Buf()], [Buf(), Buf()]
                r = [self.take32(512) for _ in range(2)]
                u = [self.take16(2, 512) for _ in range(2)]
                Br, Bu = [Buf(), Buf()], [Buf(), Buf()]
                nev = 0
                for g in range(64):
                    b = g % 2
                    self.LDW(wu[b], (w_up[l, :, 0:256] if tiny else w_up[l, :, g * 256:(g + 1) * 256]).rearrange("(kc p) n -> p kc n", p=128), [Bwu[b]], "wu%d" % b)
                    self.LDW(wd[b], (w_down[l, 0:256, :] if tiny else w_down[l, g * 256:(g + 1) * 256, :]).rearrange("(kc p) n -> p kc n", p=128), [Bwd[b]], "wd%d" % b)
                    for ch in range(2):
                        pb = (2 * g + ch) % 2
                        for kc in range(32):
                            self.MM(ps[pb][:, :], wu[b][:, kc, ch * 128:(ch + 1) * 128], h2T[:, kc, :], kc == 0, kc == 31, [Bwu[b]] + Bh2, [PB[pb]])
                        k = ch
                        self.ACT(r[k], ps[pb][:, :], AF.Relu, [PB[pb]], [Br[k]])
                        self.ACT(u[b][:, ch, :], r[k], AF.Square, [Br[k]], [Bu[b]])
                    for tt in range(4):
                        for nt in range(8):
                            pb = 2 + nev % 6
                            nev += 1
                            for ch in range(2):
                                self.MM(ps[pb][:, :], u[b][:, ch, tt * 128:(tt + 1) * 128], wd[b][:, ch, nt * 512:(nt + 1) * 512], ch == 0, ch == 1,
                                        [Bu[b], Bwd[b]], [PB[pb]])
                            dst = facc[:, tt, nt * 512:(nt + 1) * 512]
                            if g == 0:
                                self.CP(dst, ps[pb][:, :], [PB[pb]], [Bfa[tt]], eng="dve")
                            else:
                                self.TT(dst, dst, ps[pb][:, :], ALU.add, [PB[pb], Bfa[tt]], [Bfa[tt]])
                P.barrier()
                self.off = mark
                GTbc = self.take32(D)
                BGT = Buf()
                xt = [self.take32(D) for _ in range(2)]
                Bxt = [Buf(), Buf()]
                self.LD(GTbc, modrow[l, 5:6, :].partition_broadcast(128), [BGT], "e_g", rd=[Bmod[l]])
                for tt in range(4):
                    t = mt * 4 + tt
                    b = tt % 2
                    self.LD(xt[b], y[t * 128:(t + 1) * 128, :], [Bxt[b]], "e_x%d" % b, rd=By)
                    self.TT(facc[:, tt, :], facc[:, tt, :], GTbc, ALU.mult, [Bfa[tt], BGT], [Bfa[tt]])
                    self.TT(xt[b], xt[b], facc[:, tt, :], ALU.add, [Bxt[b], Bfa[tt]], [Bxt[b]], eng=PL)
                    self.ST(y[t * 128:(t + 1) * 128, :], xt[b], [Bxt[b]], By, "e_o%d" % b)
                phase_reset()

        stop = self.stop_after
        done = False
        for g in range(NGc):
            x_in, cv, cckv, ckr = xd[g], cvd[g], cckvd[g], ckrd[g]
            s0 = [s0d[0][g], s0d[1][g]]
            y, nckv, nkr = yd[g], nckvd[g], nkrd[g]
            nst = [nstd[0][g], nstd[1][g]]
            By = Byall[g]
            self.LD(rope[0:64], ropedd[g].rearrange("p (a b) -> p a b", a=2), [Bc], "c2")
            self.LD(keep, keepdd[g], [Bc], "c4")
            self.LD(cvt, cvd[g], [Bc], "c5")
            self.LD(kraug[64:68, :], fkdd[g], [Bc], "c6")
            self.LD(qraug[64:68, :], eqdd[g], [Bc], "c7")
            self.ACT(sT, cvt, AF.Silu, [Bc], [Bc])
            P.barrier()
            for l in range(DEPTH):
                src = x_in if l == 0 else y
                src_bufs = [] if l == 0 else By
                self.LD(gfm, gfmd[l], [Bgl], "p0")
                self.LD(wg2[0:33, :], wg2d[l], [Bgl], "p1")
                self.LD(gnbc, glan[l:l + 1, :].partition_broadcast(128), [Bgl], "p2")
                phase_ada(l)
                phase_reset()
                if stop == ("ada", l):
                    done = True
                    break
                hT = self.take16(32, T)
                BhT = [Buf() for _ in range(8)]
                mark_h = self.off
                phase_norm(l, 0, src, src_bufs, hT, BhT, list(range(8)))
                P.barrier()
                if stop == ("a", l):
                    done = True
                    break
                self.off = mark_h
                st = {}
                st["qnT"] = self.take16(8, T)
                st["ckvnT"] = self.take16(4, 1536)
                st["glrT"] = self.take32(T)
                st["rmla"] = self.take32(8)
                st["Bq"] = [Buf() for _ in range(8)]
                st["Bckv"], st["Bglr"], st["Brmla"] = Buf(), Buf(), Buf()
                mark_p = self.off
                st["ckvraw"] = self.take32(4, T)
                st["krraw"] = self.take32(T)
                st["sqq"] = self.take32(T)
                st["sqc"] = self.take32(T)
                st["sqk"] = self.take32(T)
                st["Bcr"] = [Buf() for _ in range(4)]
                st["Bkraw"], st["Bsqk"] = Buf(), Buf()
                st["Bsqq"], st["Bsqc"] = [Buf(), Buf()], [Buf(), Buf()]
                mark_b1 = self.off
                phase_b1(l, hT, BhT, st)
                P.barrier()
                if stop == ("b1", l):
                    done = True
                    break
                self.off = mark_b1
                phase_b2(l, st)
                P.barrier()
                if stop == ("b2", l):
                    done = True
                    break
                hT_start = mark_h - 16 * T
                self.off = hT_start
                OT = self.take16(16, T)
                BOT = Buf()
                self.hreg = self.off
                self.off = mark_p
                phase_b4(l, st, OT, BOT)
                P.barrier()
                if stop == ("b4", l):
                    done = True
                    break
                self.off = mark_p
                phase_d(l, 0, src, l > 0, OT, BOT, st)
                P.barrier()
                if stop == ("d0", l):
                    done = True
                    break
                self.off = mark_p
                phase_c(l, st, OT, BOT)
                P.barrier()
                if stop == ("c", l):
                    done = True
                    break
                self.off = mark_p
                phase_d(l, 1, y, True, OT, BOT, st)
                phase_reset()
                if stop == ("d1", l):
                    done = True
                    break
                phase_e(l)
                if stop == ("e", l):
                    done = True
                    break
            if done:
                break
        P.finalize([b for bb in Byall for b in bb] + Bout + Bmod + [Bpj])
        P.emit(ctx)
        return nc


def _consts():
    ident = np.eye(128, dtype=np.float32)
    j = np.arange(128)[:, None]
    i = np.arange(128)[None, :]
    tm = np.stack([(j <= i), (j >= i), (j > i), (j < i)], axis=1).astype(np.float32)
    rt = np.zeros((64, 64), np.float32)
    for m in range(32):
        rt[m + 32, m] = -1.0
    for m in range(32, 64):
        rt[m - 32, m] = 1.0
    return ident, tm.reshape(128, 512), rt


def _rope_tables(sample):
    if not sample:
        return np.concatenate([np.ones((64, 1024), np.float32), np.zeros((64, 1024), np.float32)], axis=1)
    t = np.arange(1024)
    row = (t // 64).astype(np.float32)
    col = (t % 64).astype(np.float32)
    inv = np.power(np.float32(10000.0), -np.arange(16, dtype=np.float32) / np.float32(16)).astype(np.float32)
    ang = np.concatenate([row[:, None] * inv, col[:, None] * inv], axis=-1).astype(np.float32)
    cos = np.cos(ang).astype(np.float32).T
    sin = np.sin(ang).astype(np.float32).T
    cos2 = np.concatenate([cos, cos], axis=0)
    sin2 = np.concatenate([sin, sin], axis=0)
    return np.ascontiguousarray(np.concatenate([cos2, sin2], axis=1))


def _masks(sample):
    eq = np.zeros((4, 1024), np.float32)
    fk = np.zeros((4, 1536), np.float32)
    keep = np.ones((128, 16), np.float32)
    if not sample:
        for s in range(4):
            eq[s, s * 256:(s + 1) * 256] = 1.0
            fk[s, :] = -30000.0
            fk[s, s * 256:(s + 1) * 256] = 0.0
        for c in (2, 4, 6):
            keep[:, c] = 0.0
        for c in (5, 3, 1):
            keep[:, 8 + c] = 0.0
    return eq.astype(ml_dtypes.bfloat16), fk.astype(ml_dtypes.bfloat16), keep


_NC_CACHE = {}


def _get_nc(stop_after=None, debug=False, ng=None, tiny=False):
    key = (stop_after, debug, ng, tiny)
    if key not in _NC_CACHE:
        b = Builder(stop_after=stop_after, debug=debug, ng=ng, tiny=tiny)
        _NC_CACHE[key] = (b.build(), b)
    return _NC_CACHE[key]


def make_in_maps(inp, assign):
    f = lambda a: np.ascontiguousarray(np.asarray(a, dtype=np.float32))
    ident, tm, rt = _consts()
    gfm = np.zeros((DEPTH, 128, 32), np.float32)
    for l in range(DEPTH):
        gfm[l, :, 0:8] = f(inp["q_a_norm"])[l].reshape(8, 128).T
        gfm[l, :, 8:12] = f(inp["kv_a_norm"])[l].reshape(4, 128).T
        gfm[l, :, 12] = f(inp["q_norm_nope"])[l]
        gfm[l, 0:64, 13] = f(inp["q_norm_rope"])[l]
        gfm[l, :, 14] = f(inp["k_norm_nope"])[l]
        gfm[l, 0:64, 15] = f(inp["k_norm_rope"])[l]
        gfm[l, :, 16:32] = f(inp["mla_out_norm"])[l].reshape(16, 128).T
    wg2 = np.zeros((DEPTH, 33, 2048), np.float32)
    wg2[:, 0:16, 0:1024] = f(inp["w_gf2"])
    wg2[:, 16:32, 1024:2048] = f(inp["w_gb2"])
    wg2[:, 32, 0:1024] = f(inp["b_gf"])
    wg2[:, 32, 1024:2048] = f(inp["b_gb"])
    shared = {k: f(inp[k]) for k in ("w_ada", "b_ada", "norm1", "norm2", "w_in", "w_qb", "w_kvb", "gla_norm", "w_o", "w_up", "w_down")}
    shared.update({"ident": ident, "tmat": tm, "rt": rt, "gfm": gfm, "wg2": wg2})
    xp = f(inp["x_prompt"])
    xs = f(inp["x_sample"])
    maps = []
    for groups in assign:
        per = {k: [] for k in ("x", "cv", "cckv", "ckr", "s0f", "s0b", "rope", "eqm", "fkm", "keep")}
        for gi in groups:
            sample = gi < 4
            eq, fk, keep = _masks(sample)
            per["eqm"].append(eq)
            per["fkm"].append(fk)
            per["keep"].append(keep)
            per["rope"].append(_rope_tables(sample))
            if sample:
                b = gi
                per["x"].append(xs[b])
                per["cv"].append(f(inp["c"])[b].reshape(32, 128).T)
                per["cckv"].append(f(inp["cache_ckv"])[b])
                per["ckr"].append(f(inp["cache_krope"])[b])
                per["s0f"].append(f(inp["state_gla_fwd"])[b])
                per["s0b"].append(f(inp["state_gla_bwd"])[b])
            else:
                j = gi - 4
                per["x"].append(xp[4 * j:4 * j + 4].reshape(T, D))
                per["cv"].append(f(inp["c_ctx"]).reshape(32, 128).T)
                per["cckv"].append(np.zeros((DEPTH, 512, 512), np.float32))
                per["ckr"].append(np.zeros((DEPTH, 512, 64), np.float32))
                per["s0f"].append(np.zeros((DEPTH, 8, 128, 256), np.float32))
                per["s0b"].append(np.zeros((DEPTH, 8, 128, 256), np.float32))
        m = dict(shared)
        for k, v in per.items():
            m[k] = np.ascontiguousarray(np.stack(v, axis=0))
        maps.append(m)
    return maps


def kernel(**inp):
    nc, _ = _get_nc()
    assign = [[c * NG + g for g in range(NG)] for c in range(NCORES)]
    maps = make_in_maps(inp, assign)
    res = run_bass_kernel_spmd(nc, maps, core_ids=list(range(NCORES))).results
    grp = {}
    for c, groups in enumerate(assign):
        for g, gi in enumerate(groups):
            grp[gi] = {k: res[c][k][g] for k in ("y", "nckv", "nkr", "nsf", "nsb")}
    y_sample = np.stack([grp[b]["y"] for b in range(4)], axis=0)
    y_prompt = np.concatenate([grp[4 + j]["y"].reshape(4, 256, D) for j in range(4)], axis=0)
    new_ckv = np.concatenate([grp[4 + j]["nckv"].reshape(DEPTH, 4, 256, 512).transpose(1, 0, 2, 3) for j in range(4)], axis=0)
    new_krope = np.concatenate([grp[4 + j]["nkr"].reshape(DEPTH, 4, 256, 64).transpose(1, 0, 2, 3) for j in range(4)], axis=0)
    nsf = np.concatenate([grp[4 + j]["nsf"].transpose(1, 0, 2, 3, 4) for j in range(4)], axis=0)
    nsb = np.concatenate([grp[4 + j]["nsb"].transpose(1, 0, 2, 3, 4) for j in range(4)], axis=0)
    return (np.ascontiguousarray(y_prompt), np.ascontiguousarray(y_sample), np.ascontiguousarray(new_ckv),
            np.ascontiguousarray(new_krope), np.ascontiguousarray(nsf), np.ascontiguousarray(nsb))
```

```python
import numpy as np
from contextlib import ExitStack
import ml_dtypes
import concourse.bass as bass
import concourse.mybir as mybir
from concourse.bass_utils import run_bass_kernel_spmd

F32 = mybir.dt.float32
BF16 = mybir.dt.bfloat16
AF = mybir.ActivationFunctionType
ALU = mybir.AluOpType

D = 4096
T = 1024
DEPTH = 2
NIN = 7776
DFF = 16384
EPS = 1e-6
NCORES = 4
NG = 8 // NCORES
ENGS = ("sp", "act", "dve", "pool", "pe")
PL = "dve"


class Buf:
    __slots__ = ("name", "lw", "rd", "excl")

    def __init__(self, name="", excl=False):
        self.name = name
        self.lw = None
        self.rd = []
        self.excl = excl


class Op:
    __slots__ = ("eng", "emit", "deps", "dma", "slot", "pos", "sig", "cnt", "waits", "idx")


class Prog:
    def __init__(self, nc):
        self.nc = nc
        self.ops = []
        self.streams = {e: [] for e in ENGS}
        self.slot_last = {}
        self.slot_n = {}
        self.dma_since_bar = []

    def op(self, eng, emit, reads=(), writes=(), dma=False, slot=None, extra=()):
        o = Op()
        o.eng, o.emit, o.dma, o.slot = eng, emit, dma, slot
        o.idx = len(self.ops)
        deps = set(extra)
        xr = [b for b in reads if b.excl]
        if xr:
            reads = [b for b in reads if not b.excl]
            writes = list(writes) + xr
        for b in reads:
            if b.lw is not None:
                deps.add(b.lw)
        for b in writes:
            if b.lw is not None:
                deps.add(b.lw)
            deps.update(b.rd)
        if dma:
            prev = self.slot_last.get(slot)
            if prev is not None:
                deps.add(prev)
            self.slot_last[slot] = o.idx
            self.slot_n[slot] = self.slot_n.get(slot, 0) + 1
            o.cnt = self.slot_n[slot]
            self.dma_since_bar.append(o.idx)
        deps.discard(o.idx)
        o.deps = deps
        for b in reads:
            b.rd.append(o.idx)
        for b in writes:
            b.lw = o.idx
            b.rd = []
        o.pos = len(self.streams[eng])
        o.sig = False
        self.streams[eng].append(o)
        self.ops.append(o)
        return o.idx

    def dma(self, q, out, in_, reads, writes, slot):
        return self.op(q, lambda e: e.dma_start(out=out, in_=in_), reads, writes, dma=True, slot=slot)

    def barrier(self):
        lasts = []
        for e in ENGS:
            for o in reversed(self.streams[e]):
                if o.emit is not None and not o.dma:
                    lasts.append(o.idx)
                    break
        extra = lasts + list(self.dma_since_bar)
        self.dma_since_bar = []
        for e in ENGS:
            self.op(e, None, extra=extra)

    def finalize(self, final_reads):
        self.op("sp", None, reads=final_reads, writes=(), extra=list(self.dma_since_bar))
        ops = self.ops
        seen = {e: {f: -1 for f in ENGS} for e in ENGS}
        seen_dma = {e: {} for e in ENGS}
        for o in ops:
            need = {}
            needd = {}
            for d in o.deps:
                t = ops[d]
                if t.dma:
                    if seen_dma[o.eng].get(t.slot, 0) < t.cnt and needd.get(t.slot, (0, None))[0] < t.cnt:
                        needd[t.slot] = (t.cnt, t)
                else:
                    if t.emit is None:
                        continue
                    if t.eng == "pe" and o.eng == "pe":
                        continue
                    if seen[o.eng][t.eng] < t.pos and need.get(t.eng, -1) < t.pos:
                        need[t.eng] = t.pos
            o.waits = []
            for f, p in need.items():
                t = self.streams[f][p]
                t.sig = True
                seen[o.eng][f] = p
                o.waits.append(("eng", f, t))
            for k, (v, t) in needd.items():
                seen_dma[o.eng][k] = v
                o.waits.append(("dma", k, t))
        for e in ENGS:
            c = 0
            for o in self.streams[e]:
                if o.dma:
                    continue
                if o.sig:
                    c += 1
                    o.cnt = c

    def emit(self, ctx):
        nc = self.nc
        esem = {e: ctx.enter_context(nc.semaphore("s_" + e)) for e in ENGS}
        dsem = {s: ctx.enter_context(nc.semaphore("d_" + str(s))) for s in self.slot_n}
        block = ctx.enter_context(nc.Block())
        reg = {"sp": block.sync, "act": block.scalar, "dve": block.vector, "pool": block.gpsimd,
               "pe": block.tensor}

        def mk(e):
            def body(eng):
                for o in self.streams[e]:
                    for kind, k, t in o.waits:
                        if kind == "eng":
                            eng.wait_ge(esem[k], t.cnt)
                        else:
                            eng.wait_ge(dsem[k], 16 * t.cnt)
                    if o.emit is None:
                        continue
                    ins = o.emit(eng)
                    if o.dma:
                        ins.then_inc(dsem[o.slot], 16)
                    elif o.sig:
                        ins.then_inc(esem[e], 1)
            return body

        for e in ENGS:
            reg[e](mk(e))


ARENA_F32 = 50944


class Builder:
    def __init__(self, stop_after=None, debug=False, ng=None, tiny=False):
        self.tiny = tiny
        self.stop_after = stop_after
        self.debug = debug
        self.ng = NG if ng is None else ng
        nc = self.nc = bass.Bass("TRN2", target_bir_lowering=False)
        self.P = Prog(nc)
        self.ctx = ExitStack()
        self.di = {}
        self.do = {}

    def din(self, name, shape, dt=F32):
        t = self.nc.dram_tensor(name, list(shape), dt, kind="ExternalInput").ap()
        self.di[name] = t
        return t

    def dout(self, name, shape, dt=F32):
        t = self.nc.dram_tensor(name, list(shape), dt, kind="ExternalOutput").ap()
        self.do[name] = t
        return t

    def dscr(self, name, shape, dt=F32):
        if self.debug:
            return self.dout(name, shape, dt)
        return self.nc.dram_tensor(name, list(shape), dt).ap()

    def take32(self, *shape, parts=128):
        n = int(np.prod(shape))
        assert self.off + n <= ARENA_F32, (self.off, n)
        ap = self.arena[0:parts, self.off:self.off + n]
        self.off += n
        if len(shape) == 2:
            ap = ap.rearrange("p (a b) -> p a b", a=shape[0])
        elif len(shape) == 3:
            ap = ap.rearrange("p (a b c) -> p a b c", a=shape[0], b=shape[1])
        return ap

    def take16(self, *shape, parts=128):
        n = int(np.prod(shape))
        n32 = (n + 1) // 2
        assert self.off + n32 <= ARENA_F32, (self.off, n32)
        ap = self.arena16[0:parts, 2 * self.off:2 * self.off + n]
        self.off += n32
        if len(shape) == 2:
            ap = ap.rearrange("p (a b) -> p a b", a=shape[0])
        elif len(shape) == 3:
            ap = ap.rearrange("p (a b c) -> p a b c", a=shape[0], b=shape[1])
        return ap

    def ACT(self, out, in_, func, rd, wr, scale=1.0, bias=0.0, accum=None):
        if accum is None:
            self.P.op("act", lambda e: e.activation(out, in_, func, bias=bias, scale=scale), rd, wr)
        else:
            self.P.op("act", lambda e: e.activation(out, in_, func, bias=bias, scale=scale, accum_out=accum), rd, wr)

    def TT(self, out, a, b, op, rd, wr, eng="dve"):
        self.P.op(eng, lambda e: e.tensor_tensor(out, a, b, op), rd, wr)

    def STT(self, out, in0, scalar, in1, op0, op1, rd, wr):
        self.P.op("dve", lambda e: e.scalar_tensor_tensor(out, in0, scalar, in1, op0, op1), rd, wr)

    def TS(self, out, in0, s1, op0, rd, wr, eng="dve"):
        self.P.op(eng, lambda e: e.tensor_scalar(out, in0, s1, None, op0), rd, wr)

    def CP(self, out, in_, rd, wr, eng="dve"):
        if eng == "act":
            self.ACT(out, in_, AF.Copy, rd, wr)
        else:
            self.P.op(eng, lambda e: e.tensor_copy(out, in_), rd, wr)

    def MM(self, out, lhsT, rhs, start, stop, rd, wr):
        self.P.op("pe", lambda e: e.matmul(out, lhsT, rhs, start=start, stop=stop), rd, wr)

    def TR(self, out, in_, rd, wr):
        k = in_.shape[0]
        idn = self.ident[0:k, 0:k]
        self.P.op("pe", lambda e: e.transpose(out, in_, idn), rd + [self.Bconst], wr)

    def LD(self, out, in_, wr, slot, rd=()):
        self.P.dma("sp", out, in_, list(rd), list(wr), slot)

    def LDW(self, out, in_, wr, slot):
        self.P.dma("pool", out, in_, [], list(wr), slot)

    def ST(self, out, in_, rd, wr, slot):
        self.P.dma("sp", out, in_, list(rd), list(wr), slot)

    def rsqrt(self, out, in_, n, tmp, rd, wr, wtmp):
        self.ACT(tmp, in_, AF.Ln, rd, wtmp, scale=1.0 / n, bias=self.epsc[0:in_.shape[0], :])
        self.ACT(out, tmp, AF.Exp, wtmp, wr, scale=-0.5)

    def build(self):
        nc = self.nc
        ctx = self.ctx
        P = self.P
        NGc = self.ng
        xd = self.din("x", [NGc, T, D])
        cvd = self.din("cv", [NGc, 128, 32])
        cckvd = self.din("cckv", [NGc, DEPTH, 512, 512])
        ckrd = self.din("ckr", [NGc, DEPTH, 512, 64])
        s0d = [self.din("s0f", [NGc, DEPTH, 8, 128, 256]), self.din("s0b", [NGc, DEPTH, 8, 128, 256])]
        identd = self.din("ident", [128, 128])
        ropedd = self.din("rope", [NGc, 64, 2048])
        rtd = self.din("rt", [64, 64])
        eqdd = self.din("eqm", [NGc, 4, 1024], BF16)
        fkdd = self.din("fkm", [NGc, 4, 1536], BF16)
        keepdd = self.din("keep", [NGc, 128, 16])
        tmatd = self.din("tmat", [128, 4 * 128])
        tiny = self.tiny
        w_ada = self.din("w_ada", [DEPTH, D, 512 if tiny else 6 * D])
        b_ada = self.din("b_ada", [DEPTH, 6 * D])
        norm1 = self.din("norm1", [DEPTH, D])
        norm2 = self.din("norm2", [DEPTH, D])
        w_in = self.din("w_in", [DEPTH, D, 256 if tiny else NIN])
        w_qb = self.din("w_qb", [DEPTH, 1024, 192 if tiny else 3072])
        w_kvb = self.din("w_kvb", [DEPTH, 512, 256 if tiny else 4096])
        gfmd = self.din("gfm", [DEPTH, 128, 32])
        wg2d = self.din("wg2", [DEPTH, 33, 2048])
        glan = self.din("gla_norm", [DEPTH, 256])
        w_o = self.din("w_o", [DEPTH, 2048 if tiny else D, 512 if tiny else D])
        w_up = self.din("w_up", [DEPTH, D, 256 if tiny else DFF])
        w_down = self.din("w_down", [DEPTH, 256 if tiny else DFF, D])

        yd = self.dout("y", [NGc, T, D])
        nckvd = self.dout("nckv", [NGc, DEPTH, T, 512])
        nkrd = self.dout("nkr", [NGc, DEPTH, T, 64])
        nstd = [self.dout("nsf", [NGc, DEPTH, 4, 8, 128, 256]), self.dout("nsb", [NGc, DEPTH, 4, 8, 128, 256])]
        modrow = self.dscr("modrow", [DEPTH, 6, D])
        pj = self.dscr("pj", [6144, T])
        Byall = [[Buf("y%d" % i) for i in range(8)] for _ in range(NGc)]
        Bout = [Buf("outs")]
        Bmod = [Buf("mod%d" % i) for i in range(DEPTH)]
        Bpj = Buf("pj")

        self.arena = ctx.enter_context(nc.sbuf_tensor("arena", [128, ARENA_F32], F32))
        self.arena16 = self.arena[:, :].bitcast(BF16)
        assert tuple(self.arena16.shape) == (128, 2 * ARENA_F32), self.arena16.shape
        self.off = 0
        ps = [ctx.enter_context(nc.psum_tensor("ps%d" % i, [128, 512], F32)) for i in range(8)]
        PB = [Buf("ps%d" % i, excl=True) for i in range(8)]

        self.Bconst = Bc = Buf("const")
        self.ident = ident = self.take32(128)
        tmat = self.take32(4, 128)
        rope = self.take32(2, 1024)
        rt = self.take32(64)
        keep = self.take32(16)
        ones32 = self.take32(128)
        epsc = self.take32(1)
        self.epsc = epsc[:, 0:1]
        onec = self.take32(1)
        cvt = self.take32(32)
        gfm = self.take32(32)
        wg2 = self.take32(2048)
        gnbc = self.take32(256)
        ones16 = self.take16(128)
        kraug = self.take16(1536)
        qraug = self.take16(1024)
        sT = self.take16(32)
        Bgl = Buf("gfm")
        Bkr = Buf("kraug")
        Bqr = Buf("qraug")
        base_off = self.off

        LDc = lambda out, in_, slot: self.LD(out, in_, [Bc], slot)
        LDc(ident, identd, "c0")
        LDc(tmat, tmatd.rearrange("p (a b) -> p a b", a=4), "c1")
        LDc(rt[0:64], rtd, "c3")
        P.op("dve", lambda e: e.memset(ones32, 1.0), [], [Bc])
        P.op("dve", lambda e: e.memset(ones16, 1.0), [], [Bc])
        P.op("dve", lambda e: e.memset(epsc, EPS), [], [Bc])
        P.op("dve", lambda e: e.memset(onec, 1.0), [], [Bc])
        P.barrier()
        cos2 = rope[0:64, 0, :]
        sin2 = rope[0:64, 1, :]
        one1 = onec[:, 0:1]

        def phase_reset():
            P.barrier()
            self.off = base_off

        def phase_ada(l):
            slab = [self.take16(32, 512) for _ in range(2)]
            Bsl = [Buf(), Buf()]
            modseg = self.take32(D)
            bseg = self.take32(D)
            nrow = self.take32(D)
            Bms, Bbs, Bnr = Buf(), Buf(), Buf()
            it = 0
            for s in range(6):
                self.LD(bseg[0:1, :], b_ada[l:l + 1, s * D:(s + 1) * D], [Bbs], "a_b")
                if s in (1, 4):
                    nr = norm1 if s == 1 else norm2
                    self.LD(nrow[0:1, :], nr[l:l + 1, :], [Bnr], "a_n")
                for nt in range(8):
                    c0 = s * D + nt * 512
                    b = it % 2
                    self.LDW(slab[b], (w_ada[l, :, 0:512] if tiny else w_ada[l, :, c0:c0 + 512]).rearrange("(kc p) n -> p kc n", p=128), [Bsl[b]], "wa%d" % b)
                    pb = it % 4
                    for kc in range(32):
                        self.MM(ps[pb][0:1, :], sT[:, kc:kc + 1], slab[b][:, kc, :], kc == 0, kc == 31, [Bsl[b], Bc], [PB[pb]])
                    self.TT(modseg[0:1, nt * 512:(nt + 1) * 512], ps[pb][0:1, :], bseg[0:1, nt * 512:(nt + 1) * 512], ALU.add,
                            [PB[pb], Bbs], [Bms])
                    it += 1
                if s in (1, 4):
                    self.STT(modseg[0:1, :], modseg[0:1, :], 1.0, nrow[0:1, :], ALU.add, ALU.mult, [Bms, Bnr], [Bms])
                self.ST(modrow[l, s:s + 1, :], modseg[0:1, :], [Bms], [Bmod[l]], "a_o")

        def phase_norm(l, which, src, src_bufs, hT, BhT, tiles):
            so = 0 if which == 0 else 3
            Gbc = self.take32(D)
            SHbc = self.take32(D)
            xt = [self.take32(D) for _ in range(2)]
            ht = self.take32(D)
            junk = self.take16(D)
            ss = self.take32(8)
            lt = self.take32(8)
            rs = self.take32(8)
            BG, Bxt, Bht, Bss = Buf(), [Buf(), Buf()], Buf(), [Buf() for _ in range(8)]
            self.LD(Gbc, modrow[l, so + 1:so + 2, :].partition_broadcast(128), [BG], "n_g", rd=[Bmod[l]])
            self.LD(SHbc, modrow[l, so:so + 1, :].partition_broadcast(128), [BG], "n_s", rd=[Bmod[l]])
            for j, t in enumerate(tiles):
                b = j % 2
                self.LD(xt[b], src[t * 128:(t + 1) * 128, :], [Bxt[b]], "n_x%d" % b, rd=src_bufs)
                self.ACT(junk, xt[b], AF.Square, [Bxt[b]], [Bss[j]], accum=ss[:, j:j + 1])
                self.rsqrt(rs[:, j:j + 1], ss[:, j:j + 1], D, lt[:, j:j + 1], [Bss[j], Bc], [Bss[j]], [Bss[j]])
                self.STT(ht, xt[b], rs[:, j:j + 1], Gbc, ALU.mult, ALU.mult, [Bxt[b], Bss[j], BG], [Bht])
                self.TT(ht, ht, SHbc, ALU.add, [Bht, BG], [Bht], eng=PL)
                for g in range(8):
                    pb = g
                    for q in range(4):
                        kc = 4 * g + q
                        self.TR(ps[pb][:, q * 128:(q + 1) * 128], ht[:, kc * 128:(kc + 1) * 128], [Bht], [PB[pb]])
                    self.CP(hT[:, 4 * g:4 * g + 4, j * 128:(j + 1) * 128], ps[pb][:, :].rearrange("p (a b) -> p a b", a=4),
                            [PB[pb]], [BhT[j]], eng=("act" if g % 2 else "dve"))

        def phase_b1(l, hT, BhT, st):
            wsl = [self.take16(32, 256) for _ in range(2)]
            Bws = [Buf(), Buf()]
            tq = [self.take32(512) for _ in range(2)]
            Btq = [Buf(), Buf()]
            stage = [self.take32(512) for _ in range(4)]
            Bstg = [Buf() for _ in range(4)]
            segs = [("q", 0, 1024), ("ckv", 1024, 512), ("kr", 1536, 64), ("gq", 1600, 1024), ("gk", 2624, 1024),
                    ("gv", 3648, 2048), ("gate", 5696, 32), ("og", 5728, 2048)]
            si = 0
            ei = 0
            for name, c0, n in segs:
                for o in range(0, n, 256):
                    w = min(256, n - o)
                    b = si % 2
                    self.LDW(wsl[b][:, :, 0:w], (w_in[l, :, 0:w] if tiny else w_in[l, :, c0 + o:c0 + o + w]).rearrange("(kc p) n -> p kc n", p=128),
                             [Bws[b]], "wi%d" % b)
                    nch = (w + 127) // 128
                    for ch in range(nch):
                        m = min(128, w - ch * 128)
                        for half in range(2):
                            pb = b * 4 + ch * 2 + half
                            for kc in range(32):
                                self.MM(ps[pb][0:m, :], wsl[b][:, kc, ch * 128:ch * 128 + m], hT[:, kc, half * 512:(half + 1) * 512],
                                        kc == 0, kc == 31, [Bws[b]] + BhT, [PB[pb]])
                    for ch in range(nch):
                        m = min(128, w - ch * 128)
                        col = o + ch * 128
                        c = col // 128
                        for half in range(2):
                            pb = b * 4 + ch * 2 + half
                            hs = slice(half * 512, (half + 1) * 512)
                            src = ps[pb][0:m, :]
                            e1 = "act" if (ei % 2 or name in ("q", "ckv", "kr")) else "dve"
                            ei += 1
                            if name == "q":
                                self.CP(st["qnT"][:, c, hs], src, [PB[pb]], [st["Bq"][c]], eng=e1)
                                if c == 0:
                                    self.ACT(st["sqq"][:, hs], src, AF.Square, [PB[pb]], [st["Bsqq"][half]])
                                else:
                                    k = ei % 2
                                    self.ACT(tq[k], src, AF.Square, [PB[pb]], [Btq[k]])
                                    self.TT(st["sqq"][:, hs], st["sqq"][:, hs], tq[k], ALU.add, [Btq[k], st["Bsqq"][half]],
                                            [st["Bsqq"][half]], eng=PL)
                            elif name == "ckv":
                                self.CP(st["ckvraw"][:, c, hs], src, [PB[pb]], [st["Bcr"][c]], eng=e1)
                                if c == 0:
                                    self.ACT(st["sqc"][:, hs], src, AF.Square, [PB[pb]], [st["Bsqc"][half]])
                                else:
                                    k = ei % 2
                                    self.ACT(tq[k], src, AF.Square, [PB[pb]], [Btq[k]])
                                    self.TT(st["sqc"][:, hs], st["sqc"][:, hs], tq[k], ALU.add, [Btq[k], st["Bsqc"][half]],
                                            [st["Bsqc"][half]], eng=PL)
                            elif name == "kr":
                                self.CP(st["krraw"][0:64, hs], src, [PB[pb]], [st["Bkraw"]], eng=e1)
                                self.ACT(st["sqk"][0:64, hs], src, AF.Square, [PB[pb]], [st["Bsqk"]])
                            elif name == "gate":
                                self.CP(st["glrT"][0:32, hs], src, [PB[pb]], [st["Bglr"]], eng=e1)
                            else:
                                base = {"gq": 0, "gk": 1024, "gv": 2048, "og": 4096}[name]
                                k = ei % 4
                                self.CP(stage[k], src, [PB[pb]], [Bstg[k]], eng=e1)
                                self.ST(pj[base + col:base + col + 128, hs], stage[k], [Bstg[k]], [Bpj], "pj%d" % k)
                    si += 1

        def phase_b2(l, st):
            qnT, ckvraw, krraw, ckvnT = st["qnT"], st["ckvraw"], st["krraw"], st["ckvnT"]
            rq = self.take32(T)
            tmp = self.take32(T)
            ostage = [self.take32(512) for _ in range(2)]
            kstage = self.take32(8, 64)
            cst = self.take32(4, 512)
            cks = self.take32(4, 64)
            t1 = self.take32(512)
            t2 = self.take32(512)
            Brq, Btmp, Bos, Bks, Bcst, Bcks, Bt1, Bt2 = Buf(), Buf(), [Buf(), Buf()], Buf(), Buf(), Buf(), Buf(), Buf()
            self.LD(cst, cckv[l].rearrange("(t p) f -> p t f", p=128), [Bcst], "b2c")
            self.LD(cks, ckr[l].rearrange("(t p) f -> p t f", p=128), [Bcks], "b2k")
            for half in range(2):
                hs = slice(half * 512, (half + 1) * 512)
                self.MM(ps[half][:, :], ones32, st["sqq"][:, hs], True, True, [Bc, st["Bsqq"][half]], [PB[half]])
                self.rsqrt(rq[:, hs], ps[half][:, :], 1024, tmp[:, hs], [PB[half], Bc], [Brq], [Btmp])
            for c in range(8):
                self.STT(qnT[:, c, :], qnT[:, c, :], gfm[:, c:c + 1], rq, ALU.mult, ALU.mult, [st["Bq"][c], Bgl, Brq], [st["Bq"][c]])
            for half in range(2):
                hs = slice(half * 512, (half + 1) * 512)
                self.MM(ps[2 + half][:, :], ones32, st["sqc"][:, hs], True, True, [Bc, st["Bsqc"][half]], [PB[2 + half]])
                self.rsqrt(rq[:, hs], ps[2 + half][:, :], 512, tmp[:, hs], [PB[2 + half], Bc, Brq], [Brq], [Btmp])
            for c in range(4):
                self.STT(ckvraw[:, c, :], ckvraw[:, c, :], gfm[:, 8 + c:9 + c], rq, ALU.mult, ALU.mult, [st["Bcr"][c], Bgl, Brq],
                         [st["Bcr"][c]])
                self.CP(ckvnT[:, c, 0:T], ckvraw[:, c, :], [st["Bcr"][c]], [st["Bckv"]], eng="act")
            for t in range(8):
                pb = 4 + t % 4
                for c in range(4):
                    self.TR(ps[pb][:, c * 128:(c + 1) * 128], ckvraw[:, c, t * 128:(t + 1) * 128], [st["Bcr"][c]], [PB[pb]])
                k = t % 2
                self.CP(ostage[k], ps[pb][:, :], [PB[pb]], [Bos[k]], eng=("act" if t % 2 else "dve"))
                self.ST(nckv[l, t * 128:(t + 1) * 128, :], ostage[k], [Bos[k]], Bout, "b2o%d" % k)
            for half in range(2):
                hs = slice(half * 512, (half + 1) * 512)
                self.MM(ps[half][0:64, :], ones32[0:64, 0:64], st["sqk"][0:64, hs], True, True, [Bc, st["Bsqk"]], [PB[half]])
                self.rsqrt(rq[0:64, hs], ps[half][0:64, :], 64, tmp[0:64, hs], [PB[half], Bc, Brq], [Brq], [Btmp])
            self.STT(krraw[0:64, :], krraw[0:64, :], gfm[0:64, 15:16], rq[0:64, :], ALU.mult, ALU.mult, [st["Bkraw"], Bgl, Brq],
                     [st["Bkraw"]])
            for t in range(8):
                self.TR(ps[2][:, t * 64:(t + 1) * 64], krraw[0:64, t * 128:(t + 1) * 128], [st["Bkraw"]], [PB[2]])
            self.CP(kstage, ps[2][:, :].rearrange("p (a b) -> p a b", a=8), [PB[2]], [Bks])
            self.ST(nkr[l].rearrange("(t p) d -> p t d", p=128), kstage, [Bks], Bout, "b2r")
            for half in range(2):
                hs = slice(half * 512, (half + 1) * 512)
                self.MM(ps[3][0:64, :], rt[0:64, :], krraw[0:64, hs], True, True, [Bc, st["Bkraw"]], [PB[3]])
                self.TT(t1[0:64, :], ps[3][0:64, :], sin2[:, hs], ALU.mult, [PB[3], Bc], [Bt1])
                self.TT(t2[0:64, :], krraw[0:64, hs], cos2[:, hs], ALU.mult, [st["Bkraw"], Bc], [Bt2], eng=PL)
                self.TT(kraug[0:64, hs], t1[0:64, :], t2[0:64, :], ALU.add, [Bt1, Bt2], [Bkr])
            for tt in range(4):
                pb = 4 + tt
                for c in range(4):
                    self.TR(ps[pb][:, c * 128:(c + 1) * 128], cst[:, tt, c * 128:(c + 1) * 128], [Bcst], [PB[pb]])
                self.CP(ckvnT[:, :, T + tt * 128:T + (tt + 1) * 128], ps[pb][:, :].rearrange("p (a b) -> p a b", a=4), [PB[pb]],
                        [st["Bckv"]], eng=("act" if tt % 2 else "dve"))
            for tt in range(4):
                self.TR(ps[0][0:64, tt * 128:(tt + 1) * 128], cks[:, tt, :], [Bcks], [PB[0]])
            self.CP(kraug[0:64, T:T + 512], ps[0][0:64, :], [PB[0]], [Bkr])

        def phase_b4(l, st, OT, BOT):
            qnT, ckvnT = st["qnT"], st["ckvnT"]
            Bqall = st["Bq"]
            wq = [self.take16(8, 192) for _ in range(2)]
            wkv = [self.take16(4, 256) for _ in range(2)]
            Bwq, Bwkv = [Buf(), Buf()], [Buf(), Buf()]
            qhT = self.take16(T)
            khT = self.take16(1536)
            vh = self.take16(12, 128)
            sqb = [self.take16(512) for _ in range(2)]
            pT = [self.take16(512) for _ in range(3)]
            tln = self.take32(512)
            rn = self.take32(512)
            qrn = self.take32(512)
            t1 = self.take32(512)
            t2 = self.take32(512)
            rcp = self.take32(512)
            oh = self.take32(512)
            tq = self.take32(512)
            sqo = self.take32(T)
            rmla = st["rmla"]
            Bqh, Bkh, Bvh, Bsqb, BpT = Buf(), Buf(), Buf(), [Buf(), Buf()], [Buf() for _ in range(3)]
            Btln, Brn, Bqrn, Bt1, Bt2, Brcp, Boh, Btq, Bsqo = [Buf() for _ in range(9)]
            qi = 0

            def normstat(src, m, pbs, n):
                nonlocal qi
                k = qi % 2
                qi += 1
                self.ACT(sqb[k][0:m, :], src, AF.Square, [pbs], [Bsqb[k]])
                self.MM(ps[3][0:m, :], ones16[0:m, 0:m], sqb[k][0:m, :], True, True, [Bc, Bsqb[k]], [PB[3]])
                self.rsqrt(rn[0:m, :], ps[3][0:m, :], n, tln[0:m, :], [PB[3], Bc], [Brn], [Btln])

            for h in range(16):
                hb = h % 2
                self.LDW(wq[hb], (w_qb[l, :, 0:192] if tiny else w_qb[l, :, h * 192:(h + 1) * 192]).rearrange("(kc p) n -> p kc n", p=128), [Bwq[hb]], "wq%d" % hb)
                self.LDW(wkv[hb], (w_kvb[l, :, 0:256] if tiny else w_kvb[l, :, h * 256:(h + 1) * 256]).rearrange("(kc p) n -> p kc n", p=128), [Bwkv[hb]],
                         "wk%d" % hb)
                for half in range(2):
                    hs = slice(half * 512, (half + 1) * 512)
                    pn = half
                    for kc in range(8):
                        self.MM(ps[pn][:, :], wq[hb][:, kc, 0:128], qnT[:, kc, hs], kc == 0, kc == 7, [Bwq[hb]] + Bqall, [PB[pn]])
                    for kc in range(8):
                        self.MM(ps[2][0:64, :], wq[hb][:, kc, 128:192], qnT[:, kc, hs], kc == 0, kc == 7, [Bwq[hb]] + Bqall, [PB[2]])
                    normstat(ps[pn][:, :], 128, PB[pn], 128)
                    self.STT(qhT[:, hs], ps[pn][:, :], gfm[:, 12:13], rn, ALU.mult, ALU.mult, [PB[pn], Bgl, Brn], [Bqh])
                    normstat(ps[2][0:64, :], 64, PB[2], 64)
                    self.STT(qrn[0:64, :], ps[2][0:64, :], gfm[0:64, 13:14], rn[0:64, :], ALU.mult, ALU.mult, [PB[2], Bgl, Brn], [Bqrn])
                    self.MM(ps[3][0:64, :], rt[0:64, :], qrn[0:64, :], True, True, [Bc, Bqrn], [PB[3]])
                    self.TT(t1[0:64, :], ps[3][0:64, :], sin2[:, hs], ALU.mult, [PB[3], Bc], [Bt1])
                    self.TT(t2[0:64, :], qrn[0:64, :], cos2[:, hs], ALU.mult, [Bqrn, Bc], [Bt2], eng=PL)
                    self.TT(qraug[0:64, hs], t1[0:64, :], t2[0:64, :], ALU.add, [Bt1, Bt2], [Bqr])
                for k3 in range(3):
                    pk = k3 % 2
                    ks = slice(k3 * 512, (k3 + 1) * 512)
                    for kc in range(4):
                        self.MM(ps[pk][:, :], wkv[hb][:, kc, 0:128], ckvnT[:, kc, ks], kc == 0, kc == 3, [Bwkv[hb], st["Bckv"]], [PB[pk]])
                    normstat(ps[pk][:, :], 128, PB[pk], 128)
                    self.STT(khT[:, ks], ps[pk][:, :], gfm[:, 14:15], rn, ALU.mult, ALU.mult, [PB[pk], Bgl, Brn], [Bkh])
                for g in range(3):
                    for j in range(4):
                        kt = 4 * g + j
                        for kc in range(4):
                            self.MM(ps[2][:, j * 128:(j + 1) * 128], ckvnT[:, kc, kt * 128:(kt + 1) * 128], wkv[hb][:, kc, 128:256],
                                    kc == 0, kc == 3, [Bwkv[hb], st["Bckv"]], [PB[2]])
                    self.CP(vh[:, 4 * g:4 * g + 4, :], ps[2][:, :].rearrange("p (a b) -> p a b", a=4), [PB[2]], [Bvh], eng="act")
                for half in range(2):
                    hs = slice(half * 512, (half + 1) * 512)
                    for kt in range(12):
                        pss = 4 + kt % 2
                        pk = kt % 3
                        self.MM(ps[pss][:, :], khT[:, kt * 128:(kt + 1) * 128], qhT[:, hs], True, False, [Bkh, Bqh], [PB[pss]])
                        self.MM(ps[pss][:, :], kraug[0:68, kt * 128:(kt + 1) * 128], qraug[0:68, hs], False, True, [Bkr, Bqr, Bc], [PB[pss]])
                        self.ACT(pT[pk], ps[pss][:, :], AF.Exp, [PB[pss]], [BpT[pk]], scale=192.0 ** -0.5)
                        self.MM(ps[6][:, :], vh[:, kt, :], pT[pk], kt == 0, kt == 11, [Bvh, BpT[pk]], [PB[6]])
                        self.MM(ps[7][:, :], ones16, pT[pk], kt == 0, kt == 11, [Bc, BpT[pk]], [PB[7]])
                    self.P.op("dve", lambda e: e.reciprocal(rcp, ps[7][:, :]), [PB[7]], [Brcp])
                    self.TT(oh, ps[6][:, :], rcp, ALU.mult, [PB[6], Brcp], [Boh])
                    if h == 0:
                        self.ACT(sqo[:, hs], oh, AF.Square, [Boh], [Bsqo])
                    else:
                        self.ACT(tq, oh, AF.Square, [Boh], [Btq])
                        self.TT(sqo[:, hs], sqo[:, hs], tq, ALU.add, [Btq, Bsqo], [Bsqo], eng=PL)
                    self.TS(OT[:, h, hs], oh, gfm[:, 16 + h:17 + h], ALU.mult, [Boh, Bgl], [BOT])
            for t in range(8):
                self.MM(ps[3][:, t:t + 1], sqo[:, t * 128:(t + 1) * 128], ones32[:, 0:1], True, True, [Bsqo, Bc], [PB[3]])
            self.rsqrt(rmla, ps[3][:, 0:8], 2048, tln[:, 0:8], [PB[3], Bc], [st["Brmla"]], [Btln])

        def phase_c(l, st, OT, BOT):
            glrT = st["glrT"]
            sp = self.take32(2, 8, 128)
            Bsp = [[Buf() for _ in range(8)] for _ in range(2)]
            ld = self.take32(6, T)
            Bld = Buf()
            ktok = self.take32(8, 128)
            ogtok = self.take32(8, 256)
            oacc = self.take32(8, 256)
            vtok = self.take16(8, 256)
            Bkt, Bogt, Boa, Bvt = Buf(), Buf(), [Buf() for _ in range(8)], Buf()
            S = [[self.take32(256) for _ in range(2)] for _ in range(2)]
            Sbf = [[self.take16(256) for _ in range(2)] for _ in range(2)]
            BS = [[Buf(), Buf()], [Buf(), Buf()]]
            BSb = [[Buf(), Buf()], [Buf(), Buf()]]
            E = [self.take32(128) for _ in range(2)]
            Ei = [self.take32(128) for _ in range(2)]
            Ed = [self.take32(128) for _ in range(2)]
            ebk = [self.take32(1) for _ in range(2)]
            Qp = [self.take16(128) for _ in range(2)]
            Kpp = [self.take16(128) for _ in range(2)]
            Kp = [self.take16(128) for _ in range(2)]
            ATm = [self.take16(128) for _ in range(2)]
            BE, BEi, BEd, Bebk, BQp, BKpp, BKp, BATm = [[Buf(), Buf()] for _ in range(8)]
            ssg = self.take32(8)
            ltg = self.take32(8)
            rg = self.take32(8)
            sg = self.take32(256)
            tg = self.take32(256)
            Bssg, Bsg, Btg = [Buf() for _ in range(8)], Buf(), Buf()
            junk = self.take16(256)
            gtmp = self.take32(512)
            Bgt = Buf()
            bA = [0, 2]
            bB = [1, 3]
            PA = [[PB[bA[d]]] * 3 for d in range(2)]
            PBo = [[PB[bB[d]]] * 2 for d in range(2)]
            P.op("dve", lambda e: e.memset(glrT[32:33, :], 1.0), [st["Bglr"]], [st["Bglr"]])
            QS = 128.0 ** -0.5
            for h in range(8):
                self.LD(ld[:, 0, :], pj[h * 128:(h + 1) * 128, :], [Bld], "c_l0", rd=[Bpj])
                self.LD(ld[:, 1, :], pj[1024 + h * 128:1024 + (h + 1) * 128, :], [Bld], "c_l1", rd=[Bpj])
                self.LD(ld[:, 2:4, :], pj[2048 + h * 256:2048 + (h + 1) * 256, :].rearrange("(a p) t -> p a t", p=128), [Bld], "c_l2", rd=[Bpj])
                self.LD(ld[:, 4:6, :], pj[4096 + h * 256:4096 + (h + 1) * 256, :].rearrange("(a p) t -> p a t", p=128), [Bld], "c_l3", rd=[Bpj])
                qT = ld[:, 0, :]
                kT = ld[:, 1, :]
                for d in range(2):
                    for g in range(2):
                        pb = 4 + g
                        for j in range(4):
                            c = 4 * g + j
                            self.MM(ps[pb][:, j * 128:(j + 1) * 128], glrT[0:33, c * 128:(c + 1) * 128],
                                    wg2[0:33, d * 1024 + h * 128:d * 1024 + (h + 1) * 128], True, True, [st["Bglr"], Bgl], [PB[pb]])
                        self.ACT(gtmp, ps[pb][:, :], AF.Exp, [PB[pb]], [Bgt], scale=-1.0)
                        self.ACT(sp[:, d, 4 * g:4 * g + 4, :], gtmp.rearrange("p (a b) -> p a b", a=4), AF.Ln, [Bgt, Bc],
                                 [Bsp[d][4 * g + j] for j in range(4)], bias=one1)
                for g in range(2):
                    pb = 4 + g
                    for j in range(4):
                        c = 4 * g + j
                        self.TR(ps[pb][:, j * 128:(j + 1) * 128], kT[:, c * 128:(c + 1) * 128], [Bld], [PB[pb]])
                    self.CP(ktok[:, 4 * g:4 * g + 4, :], ps[pb][:, :].rearrange("p (a b) -> p a b", a=4), [PB[pb]], [Bkt], eng="act")
                for c in range(8):
                    pb = 6 + c % 2
                    for a in range(2):
                        self.TR(ps[pb][:, a * 128:(a + 1) * 128], ld[:, 2 + a, c * 128:(c + 1) * 128], [Bld], [PB[pb]])
                        self.TR(ps[pb][:, 256 + a * 128:256 + (a + 1) * 128], ld[:, 4 + a, c * 128:(c + 1) * 128], [Bld], [PB[pb]])
                    self.CP(vtok[:, c, :], ps[pb][:, 0:256], [PB[pb]], [Bvt], eng="dve")
                    self.CP(ogtok[:, c, :], ps[pb][:, 256:512], [PB[pb]], [Bogt], eng="act")
                cur = [0, 0]
                for d in range(2):
                    first = 0 if d == 0 else 7
                    self.LD(S[d][0], s0[d][l, h], [BS[d][0]], "c_s%d" % d)
                    self.ACT(Sbf[d][0], S[d][0], AF.Copy, [BS[d][0], Bc], [BSb[d][0]], scale=keep[:, d * 8 + first:d * 8 + first + 1])
                for step in range(8):
                    for d in range(2):
                        c = step if d == 0 else 7 - step
                        cs = slice(c * 128, (c + 1) * 128)
                        hcols = slice(0, 128)
                        a, b_ = bA[d], bB[d]
                        so, sn = cur[d], 1 - cur[d]
                        self.MM(ps[a][:, 0:128], sp[:, d, c, hcols], tmat[:, d, :], True, True, [Bsp[d][c], Bc], [PA[d][0]])
                        self.MM(ps[a][:, 128:256], tmat[:, 2 + d, :], sp[:, d, c, hcols], True, True, [Bsp[d][c], Bc], [PA[d][1]])
                        self.ACT(E[d], ps[a][:, 0:128], AF.Exp, [PA[d][0]], [BE[d]], scale=-1.0 / 16)
                        self.ACT(Ei[d], ps[a][:, 0:128], AF.Exp, [PA[d][0]], [BEi[d]], scale=1.0 / 16)
                        self.ACT(Ed[d], ps[a][:, 128:256], AF.Exp, [PA[d][1]], [BEd[d]], scale=-1.0 / 16)
                        self.STT(Qp[d], qT[:, cs], QS, E[d], ALU.mult, ALU.mult, [Bld, BE[d]], [BQp[d]])
                        self.TT(Kpp[d], kT[:, cs], Ei[d], ALU.mult, [Bld, BEi[d]], [BKpp[d]], eng=PL)
                        self.TT(Kp[d], ktok[:, c, :], Ed[d], ALU.mult, [Bkt, BEd[d]], [BKp[d]])
                        self.MM(ps[a][:, 256:384], Kpp[d], Qp[d], True, True, [BKpp[d], BQp[d]], [PA[d][2]])
                        self.TT(ATm[d], ps[a][:, 256:384], tmat[:, d, :], ALU.mult, [PA[d][2], Bc], [BATm[d]])
                        self.MM(ps[b_][:, 0:256], ATm[d], vtok[:, c, :], True, False, [BATm[d], Bvt], [PBo[d][0]])
                        self.MM(ps[b_][:, 0:256], Qp[d], Sbf[d][so], False, True, [BQp[d], BSb[d][so]], [PBo[d][0]])
                        first_writer = (d == 0) == (c < 4)
                        if first_writer:
                            self.CP(oacc[:, c, :], ps[b_][:, 0:256], [PBo[d][0]], [Boa[c]], eng="act")
                        else:
                            self.TT(oacc[:, c, :], oacc[:, c, :], ps[b_][:, 0:256], ALU.add, [PBo[d][0], Boa[c]], [Boa[c]])
                        self.MM(ps[b_][:, 256:512], Kp[d], vtok[:, c, :], True, True, [BKp[d], Bvt], [PBo[d][1]])
                        lastcol = 127 if d == 0 else 0
                        self.TT(ebk[d], E[d][:, lastcol:lastcol + 1], keep[:, d * 8 + c:d * 8 + c + 1], ALU.mult, [BE[d], Bc], [Bebk[d]])
                        self.STT(S[d][sn], S[d][so], ebk[d], ps[b_][:, 256:512], ALU.mult, ALU.add, [BS[d][so], Bebk[d], PBo[d][1]],
                                 [BS[d][sn]])
                        cur[d] = sn
                        seq_end = (c % 2 == 1) if d == 0 else (c % 2 == 0)
                        if seq_end:
                            self.ST(nst[d][l, c // 2, h], S[d][sn], [BS[d][sn]], Bout, "c_o%d" % d)
                        if step < 7:
                            nxt = c + 1 if d == 0 else c - 1
                            self.ACT(Sbf[d][sn], S[d][sn], AF.Copy, [BS[d][sn], Bc], [BSb[d][sn]], scale=keep[:, d * 8 + nxt:d * 8 + nxt + 1])
                for c in range(8):
                    self.ACT(junk, oacc[:, c, :], AF.Square, [Boa[c]], [Bssg[c]], accum=ssg[:, c:c + 1])
                    self.rsqrt(rg[:, c:c + 1], ssg[:, c:c + 1], 256, ltg[:, c:c + 1], [Bssg[c], Bc], [Bssg[c]], [Bssg[c]])
                    self.ACT(sg, ogtok[:, c, :], AF.Silu, [Bogt], [Bsg])
                    self.STT(tg, oacc[:, c, :], rg[:, c:c + 1], gnbc, ALU.mult, ALU.mult, [Boa[c], Bssg[c], Bgl], [Btg])
                    self.TT(tg, tg, sg, ALU.mult, [Btg, Bsg], [Btg], eng=PL)
                    pb = 4 + c % 2
                    for a in range(2):
                        self.TR(ps[pb][:, a * 128:(a + 1) * 128], tg[:, a * 128:(a + 1) * 128], [Btg], [PB[pb]])
                    self.CP(OT[:, 2 * h:2 * h + 2, c * 128:(c + 1) * 128], ps[pb][:, 0:256].rearrange("p (a b) -> p a b", a=2),
                            [PB[pb]], [BOT], eng="dve")

        def phase_d(l, part, src, src_is_y, OT, BOT, st):
            keep_off = self.off
            self.off = self.hreg
            wsl = [self.take16(16, 512) for _ in range(2)]
            self.off = keep_off
            Bws = [Buf(), Buf()]
            GTbc = self.take32(D)
            BGT = Buf()
            xs = [self.take32(8, 512) for _ in range(2)]
            Bxs = [Buf(), Buf()]
            tmp = [self.take32(512) for _ in range(2)]
            Btmp = [Buf(), Buf()]
            rmla = st["rmla"]
            self.LD(GTbc, modrow[l, 2:3, :].partition_broadcast(128), [BGT], "d_g", rd=[Bmod[l]])

            def load(s):
                rd = [By[s]] if src_is_y else []
                self.LD(xs[s % 2], src[:, s * 512:(s + 1) * 512].rearrange("(t p) n -> p t n", p=128), [Bxs[s % 2]], "d_x%d" % (s % 2), rd=rd)
            load(0)
            for s in range(8):
                b = s % 2
                self.LDW(wsl[b], (w_o[l, 0:2048, 0:512] if tiny else w_o[l, part * 2048:(part + 1) * 2048, s * 512:(s + 1) * 512]).rearrange("(kc p) n -> p kc n", p=128),
                         [Bws[b]], "wo%d" % b)
                if s + 1 < 8:
                    load(s + 1)
                for tt in range(8):
                    pb = tt
                    for kc in range(16):
                        self.MM(ps[pb][:, :], OT[:, kc, tt * 128:(tt + 1) * 128], wsl[b][:, kc, :], kc == 0, kc == 15, [BOT, Bws[b]], [PB[pb]])
                    k = tt % 2
                    if part == 0:
                        self.STT(tmp[k], ps[pb][:, :], rmla[:, tt:tt + 1], GTbc[:, s * 512:(s + 1) * 512], ALU.mult, ALU.mult,
                                 [PB[pb], st["Brmla"], BGT], [Btmp[k]])
                    else:
                        self.TT(tmp[k], ps[pb][:, :], GTbc[:, s * 512:(s + 1) * 512], ALU.mult, [PB[pb], BGT], [Btmp[k]])
                    self.TT(xs[b][:, tt, :], xs[b][:, tt, :], tmp[k], ALU.add, [Btmp[k], Bxs[b]], [Bxs[b]], eng=PL)
                self.ST(y[:, s * 512:(s + 1) * 512].rearrange("(t p) n -> p t n", p=128), xs[b], [Bxs[b]], [By[s]], "d_o%d" % b)

        def phase_e(l):
            for mt in range(2):
                h2T = self.take16(32, 512)
                Bh2 = [Buf() for _ in range(4)]
                fstart = self.off
                facc = self.take32(4, D)
                Bfa = [Buf() for _ in range(4)]
                mark = self.off
                self.off = fstart
                phase_norm(l, 1, y, By, h2T, Bh2, [mt * 4 + j for j in range(4)])
                P.barrier()
                self.off = mark
                wu = [self.take16(32, 256) for _ in range(2)]
                wd = [self.take16(2, D) for _ in range(2)]
                Bwu, Bwd = [Buf(), Buf()], [Buf(), Buf()]
                r = [self.take32(512) for _ in range(2)]
                u = [self.take16(2, 512) for _ in range(2)]
                Br, Bu = [Buf(), Buf()], [Buf(), Buf()]
                nev = 0
                for g in range(64):
                    b = g % 2
                    self.LDW(wu[b], (w_up[l, :, 0:256] if tiny else w_up[l, :, g * 256:(g + 1) * 256]).rearrange("(kc p) n -> p kc n", p=128), [Bwu[b]], "wu%d" % b)
                    self.LDW(wd[b], (w_down[l, 0:256, :] if tiny else w_down[l, g * 256:(g + 1) * 256, :]).rearrange("(kc p) n -> p kc n", p=128), [Bwd[b]], "wd%d" % b)
                    for ch in range(2):
                        pb = (2 * g + ch) % 2
                        for kc in range(32):
                            self.MM(ps[pb][:, :], wu[b][:, kc, ch * 128:(ch + 1) * 128], h2T[:, kc, :], kc == 0, kc == 31, [Bwu[b]] + Bh2, [PB[pb]])
                        k = ch
                        self.ACT(r[k], ps[pb][:, :], AF.Relu, [PB[pb]], [Br[k]])
                        self.ACT(u[b][:, ch, :], r[k], AF.Square, [Br[k]], [Bu[b]])
                    for tt in range(4):
                        for nt in range(8):
                            pb = 2 + nev % 6
                            nev += 1
                            for ch in range(2):
                                self.MM(ps[pb][:, :], u[b][:, ch, tt * 128:(tt + 1) * 128], wd[b][:, ch, nt * 512:(nt + 1) * 512], ch == 0, ch == 1,
                                        [Bu[b], Bwd[b]], [PB[pb]])
                            dst = facc[:, tt, nt * 512:(nt + 1) * 512]
                            if g == 0:
                                self.CP(dst, ps[pb][:, :], [PB[pb]], [Bfa[tt]], eng="dve")
                            else:
                                self.TT(dst, dst, ps[pb][:, :], ALU.add, [PB[pb], Bfa[tt]], [Bfa[tt]])
                P.barrier()
                self.off = mark
                GTbc = self.take32(D)
                BGT = Buf()
                xt = [self.take32(D) for _ in range(2)]
                Bxt = [Buf(), Buf()]
                self.LD(GTbc, modrow[l, 5:6, :].partition_broadcast(128), [BGT], "e_g", rd=[Bmod[l]])
                for tt in range(4):
                    t = mt * 4 + tt
                    b = tt % 2
                    self.LD(xt[b], y[t * 128:(t + 1) * 128, :], [Bxt[b]], "e_x%d" % b, rd=By)
                    self.TT(facc[:, tt, :], facc[:, tt, :], GTbc, ALU.mult, [Bfa[tt], BGT], [Bfa[tt]])
                    self.TT(xt[b], xt[b], facc[:, tt, :], ALU.add, [Bxt[b], Bfa[tt]], [Bxt[b]], eng=PL)
                    self.ST(y[t * 128:(t + 1) * 128, :], xt[b], [Bxt[b]], By, "e_o%d" % b)
                phase_reset()

        stop = self.stop_after
        done = False
        for g in range(NGc):
            x_in, cv, cckv, ckr = xd[g], cvd[g], cckvd[g], ckrd[g]
            s0 = [s0d[0][g], s0d[1][g]]
            y, nckv, nkr = yd[g], nckvd[g], nkrd[g]
            nst = [nstd[0][g], nstd[1][g]]
            By = Byall[g]
            self.LD(rope[0:64], ropedd[g].rearrange("p (a b) -> p a b", a=2), [Bc], "c2")
            self.LD(keep, keepdd[g], [Bc], "c4")
            self.LD(cvt, cvd[g], [Bc], "c5")
            self.LD(kraug[64:68, :], fkdd[g], [Bc], "c6")
            self.LD(qraug[64:68, :], eqdd[g], [Bc], "c7")
            self.ACT(sT, cvt, AF.Silu, [Bc], [Bc])
            P.barrier()
            for l in range(DEPTH):
                src = x_in if l == 0 else y
                src_bufs = [] if l == 0 else By
                self.LD(gfm, gfmd[l], [Bgl], "p0")
                self.LD(wg2[0:33, :], wg2d[l], [Bgl], "p1")
                self.LD(gnbc, glan[l:l + 1, :].partition_broadcast(128), [Bgl], "p2")
                phase_ada(l)
                phase_reset()
                if stop == ("ada", l):
                    done = True
                    break
                hT = self.take16(32, T)
                BhT = [Buf() for _ in range(8)]
                mark_h = self.off
                phase_norm(l, 0, src, src_bufs, hT, BhT, list(range(8)))
                P.barrier()
                if stop == ("a", l):
                    done = True
                    break
                self.off = mark_h
                st = {}
                st["qnT"] = self.take16(8, T)
                st["ckvnT"] = self.take16(4, 1536)
                st["glrT"] = self.take32(T)
                st["rmla"] = self.take32(8)
                st["Bq"] = [Buf() for _ in range(8)]
                st["Bckv"], st["Bglr"], st["Brmla"] = Buf(), Buf(), Buf()
                mark_p = self.off
                st["ckvraw"] = self.take32(4, T)
                st["krraw"] = self.take32(T)
                st["sqq"] = self.take32(T)
                st["sqc"] = self.take32(T)
                st["sqk"] = self.take32(T)
                st["Bcr"] = [Buf() for _ in range(4)]
                st["Bkraw"], st["Bsqk"] = Buf(), Buf()
                st["Bsqq"], st["Bsqc"] = [Buf(), Buf()], [Buf(), Buf()]
                mark_b1 = self.off
                phase_b1(l, hT, BhT, st)
                P.barrier()
                if stop == ("b1", l):
                    done = True
                    break
                self.off = mark_b1
                phase_b2(l, st)
                P.barrier()
                if stop == ("b2", l):
                    done = True
                    break
                hT_start = mark_h - 16 * T
                self.off = hT_start
                OT = self.take16(16, T)
                BOT = Buf()
                self.hreg = self.off
                self.off = mark_p
                phase_b4(l, st, OT, BOT)
                P.barrier()
                if stop == ("b4", l):
                    done = True
                    break
                self.off = mark_p
                phase_d(l, 0, src, l > 0, OT, BOT, st)
                P.barrier()
                if stop == ("d0", l):
                    done = True
                    break
                self.off = mark_p
                phase_c(l, st, OT, BOT)
                P.barrier()
                if stop == ("c", l):
                    done = True
                    break
                self.off = mark_p
                phase_d(l, 1, y, True, OT, BOT, st)
                phase_reset()
                if stop == ("d1", l):
                    done = True
                    break
                phase_e(l)
                if stop == ("e", l):
                    done = True
                    break
            if done:
                break
        P.finalize([b for bb in Byall for b in bb] + Bout + Bmod + [Bpj])
        P.emit(ctx)
        return nc


def _consts():
    ident = np.eye(128, dtype=np.float32)
    j = np.arange(128)[:, None]
    i = np.arange(128)[None, :]
    tm = np.stack([(j <= i), (j >= i), (j > i), (j < i)], axis=1).astype(np.float32)
    rt = np.zeros((64, 64), np.float32)
    for m in range(32):
        rt[m + 32, m] = -1.0
    for m in range(32, 64):
        rt[m - 32, m] = 1.0
    return ident, tm.reshape(128, 512), rt


def _rope_tables(sample):
    if not sample:
        return np.concatenate([np.ones((64, 1024), np.float32), np.zeros((64, 1024), np.float32)], axis=1)
    t = np.arange(1024)
    row = (t // 64).astype(np.float32)
    col = (t % 64).astype(np.float32)
    inv = np.power(np.float32(10000.0), -np.arange(16, dtype=np.float32) / np.float32(16)).astype(np.float32)
    ang = np.concatenate([row[:, None] * inv, col[:, None] * inv], axis=-1).astype(np.float32)
    cos = np.cos(ang).astype(np.float32).T
    sin = np.sin(ang).astype(np.float32).T
    cos2 = np.concatenate([cos, cos], axis=0)
    sin2 = np.concatenate([sin, sin], axis=0)
    return np.ascontiguousarray(np.concatenate([cos2, sin2], axis=1))


def _masks(sample):
    eq = np.zeros((4, 1024), np.float32)
    fk = np.zeros((4, 1536), np.float32)
    keep = np.ones((128, 16), np.float32)
    if not sample:
        for s in range(4):
            eq[s, s * 256:(s + 1) * 256] = 1.0
            fk[s, :] = -30000.0
            fk[s, s * 256:(s + 1) * 256] = 0.0
        for c in (2, 4, 6):
            keep[:, c] = 0.0
        for c in (5, 3, 1):
            keep[:, 8 + c] = 0.0
    return eq.astype(ml_dtypes.bfloat16), fk.astype(ml_dtypes.bfloat16), keep


_NC_CACHE = {}


def _get_nc(stop_after=None, debug=False, ng=None, tiny=False):
    key = (stop_after, debug, ng, tiny)
    if key not in _NC_CACHE:
        b = Builder(stop_after=stop_after, debug=debug, ng=ng, tiny=tiny)
        _NC_CACHE[key] = (b.build(), b)
    return _NC_CACHE[key]


def make_in_maps(inp, assign):
    f = lambda a: np.ascontiguousarray(np.asarray(a, dtype=np.float32))
    ident, tm, rt = _consts()
    gfm = np.zeros((DEPTH, 128, 32), np.float32)
    for l in range(DEPTH):
        gfm[l, :, 0:8] = f(inp["q_a_norm"])[l].reshape(8, 128).T
        gfm[l, :, 8:12] = f(inp["kv_a_norm"])[l].reshape(4, 128).T
        gfm[l, :, 12] = f(inp["q_norm_nope"])[l]
        gfm[l, 0:64, 13] = f(inp["q_norm_rope"])[l]
        gfm[l, :, 14] = f(inp["k_norm_nope"])[l]
        gfm[l, 0:64, 15] = f(inp["k_norm_rope"])[l]
        gfm[l, :, 16:32] = f(inp["mla_out_norm"])[l].reshape(16, 128).T
    wg2 = np.zeros((DEPTH, 33, 2048), np.float32)
    wg2[:, 0:16, 0:1024] = f(inp["w_gf2"])
    wg2[:, 16:32, 1024:2048] = f(inp["w_gb2"])
    wg2[:, 32, 0:1024] = f(inp["b_gf"])
    wg2[:, 32, 1024:2048] = f(inp["b_gb"])
    shared = {k: f(inp[k]) for k in ("w_ada", "b_ada", "norm1", "norm2", "w_in", "w_qb", "w_kvb", "gla_norm", "w_o", "w_up", "w_down")}
    shared.update({"ident": ident, "tmat": tm, "rt": rt, "gfm": gfm, "wg2": wg2})
    xp = f(inp["x_prompt"])
    xs = f(inp["x_sample"])
    maps = []
    for groups in assign:
        per = {k: [] for k in ("x", "cv", "cckv", "ckr", "s0f", "s0b", "rope", "eqm", "fkm", "keep")}
        for gi in groups:
            sample = gi < 4
            eq, fk, keep = _masks(sample)
            per["eqm"].append(eq)
            per["fkm"].append(fk)
            per["keep"].append(keep)
            per["rope"].append(_rope_tables(sample))
            if sample:
                b = gi
                per["x"].append(xs[b])
                per["cv"].append(f(inp["c"])[b].reshape(32, 128).T)
                per["cckv"].append(f(inp["cache_ckv"])[b])
                per["ckr"].append(f(inp["cache_krope"])[b])
                per["s0f"].append(f(inp["state_gla_fwd"])[b])
                per["s0b"].append(f(inp["state_gla_bwd"])[b])
            else:
                j = gi - 4
                per["x"].append(xp[4 * j:4 * j + 4].reshape(T, D))
                per["cv"].append(f(inp["c_ctx"]).reshape(32, 128).T)
                per["cckv"].append(np.zeros((DEPTH, 512, 512), np.float32))
                per["ckr"].append(np.zeros((DEPTH, 512, 64), np.float32))
                per["s0f"].append(np.zeros((DEPTH, 8, 128, 256), np.float32))
                per["s0b"].append(np.zeros((DEPTH, 8, 128, 256), np.float32))
        m = dict(shared)
        for k, v in per.items():
            m[k] = np.ascontiguousarray(np.stack(v, axis=0))
        maps.append(m)
    return maps


def kernel(**inp):
    nc, _ = _get_nc()
    assign = [[c * NG + g for g in range(NG)] for c in range(NCORES)]
    maps = make_in_maps(inp, assign)
    res = run_bass_kernel_spmd(nc, maps, core_ids=list(range(NCORES))).results
    grp = {}
    for c, groups in enumerate(assign):
        for g, gi in enumerate(groups):
            grp[gi] = {k: res[c][k][g] for k in ("y", "nckv", "nkr", "nsf", "nsb")}
    y_sample = np.stack([grp[b]["y"] for b in range(4)], axis=0)
    y_prompt = np.concatenate([grp[4 + j]["y"].reshape(4, 256, D) for j in range(4)], axis=0)
    new_ckv = np.concatenate([grp[4 + j]["nckv"].reshape(DEPTH, 4, 256, 512).transpose(1, 0, 2, 3) for j in range(4)], axis=0)
    new_krope = np.concatenate([grp[4 + j]["nkr"].reshape(DEPTH, 4, 256, 64).transpose(1, 0, 2, 3) for j in range(4)], axis=0)
    nsf = np.concatenate([grp[4 + j]["nsf"].transpose(1, 0, 2, 3, 4) for j in range(4)], axis=0)
    nsb = np.concatenate([grp[4 + j]["nsb"].transpose(1, 0, 2, 3, 4) for j in range(4)], axis=0)
    return (np.ascontiguousarray(y_prompt), np.ascontiguousarray(y_sample), np.ascontiguousarray(new_ckv),
            np.ascontiguousarray(new_krope), np.ascontiguousarray(nsf), np.ascontiguousarray(nsb))
```
